# Optimizing a Trainium2 kernel written in Bass

```python
import jax, jax.numpy as jnp
from jax import lax
import numpy as np


D_MODEL = 2048
BATCH = 2
SEQ = 4096
DEPTH = 2
DEC_BATCH = 32
DEC_SEQ = 4
PAST_LEN = 8192
PAGE_SIZE = 128

N_A = DEPTH // 2
N_B = DEPTH - N_A
RET_HEADS = 8
RET_DK = D_MODEL // RET_HEADS
RET_DV = 2 * RET_DK
RET_CHUNK = 128
ROPE_BASE = 10000.0
DIL_PAIRS = ((128, 1), (512, 4), (2048, 16))
N_GROUPS = len(DIL_PAIRS)
DIL_HEADS = 16
DIL_HD = D_MODEL // DIL_HEADS
D_FF = ((8 * D_MODEL // 3 + 255) // 256) * 256
EPS = 1e-6
NEG = -1e30

kernel_name = 'yoco_retention_dilated_attention_step'


def rmsnorm(x, g):
    xf = x.astype(jnp.float32)
    y = xf * lax.rsqrt(jnp.mean(xf * xf, axis=-1, keepdims=True) + EPS) * g.astype(jnp.float32)
    return y.astype(x.dtype)


def rope(x, pos):
    half = x.shape[-1] // 2
    inv = ROPE_BASE ** (-jnp.arange(half, dtype=jnp.float32) / half)
    ang = pos.astype(jnp.float32)[:, None] * inv[None, :]
    cos = jnp.cos(ang)[None, :, None, :]
    sin = jnp.sin(ang)[None, :, None, :]
    x1, x2 = x[..., :half], x[..., half:]
    return jnp.concatenate([x1 * cos - x2 * sin, x1 * sin + x2 * cos], axis=-1).astype(x.dtype)


def retention_chunks(q, k, v, s0, chunk):
    B, T, H, DK = q.shape
    DV = v.shape[-1]
    nc = T // chunk
    lg = jnp.log1p(-jnp.exp2(-5.0 - jnp.arange(H, dtype=jnp.float32)))
    i = jnp.arange(chunk, dtype=jnp.float32)
    dist = i[:, None] - i[None, :]
    dmat = jnp.where(dist >= 0, jnp.exp(jnp.maximum(dist, 0.0)[None] * lg[:, None, None]), 0.0)
    qdec = jnp.exp((i + 1.0)[:, None] * lg[None, :])
    kdec = jnp.exp((chunk - 1.0 - i)[:, None] * lg[None, :])
    cdec = jnp.exp(chunk * lg)

    def split(t):
        return jnp.moveaxis(t.reshape(B, nc, chunk, H, t.shape[-1]), 1, 0)

    def step(s, blk):
        qc, kc, vc = blk
        sc = jnp.einsum('bihd,bjhd->bhij', qc, kc) * dmat
        o = (jnp.einsum('bhij,bjhe->bihe', sc, vc)
             + jnp.einsum('bihd,bhde->bihe', qc * qdec[None, :, :, None], s))
        s = s * cdec[:, None, None] + jnp.einsum('bjhd,bjhe->bhde', kc * kdec[None, :, :, None], vc)
        return s, o

    s_fin, o = lax.scan(step, s0.astype(jnp.float32), (split(q), split(k), split(v)))
    o = jnp.moveaxis(o, 0, 1).reshape(B, T, H, DV)
    return o, s_fin


def retention_mixer(x, s0, pos, chunk, w_in, gn_g, w_out):
    B, T, _ = x.shape
    qk_w = RET_HEADS * RET_DK
    v_w = RET_HEADS * RET_DV
    proj = x @ w_in
    q = proj[..., :qk_w].reshape(B, T, RET_HEADS, RET_DK)
    k = proj[..., qk_w:2 * qk_w].reshape(B, T, RET_HEADS, RET_DK)
    v = proj[..., 2 * qk_w:2 * qk_w + v_w].reshape(B, T, RET_HEADS, RET_DV)
    g = proj[..., 2 * qk_w + v_w:]
    q = rope(q, pos)
    k = rope(k, pos) * (RET_DK ** -0.5)
    o, s_fin = retention_chunks(q, k, v, s0, chunk)
    of = o.astype(jnp.float32)
    mu = jnp.mean(of, axis=-1, keepdims=True)
    var = jnp.mean(jnp.square(of - mu), axis=-1, keepdims=True)
    on = ((of - mu) * lax.rsqrt(var + EPS)).reshape(B, T, v_w) * gn_g.astype(jnp.float32)
    y = (jax.nn.silu(g) * on.astype(x.dtype)) @ w_out
    return y, s_fin


def dilated_prompt(q, k, v, window, dilation):
    B, S, H, Dh = q.shape
    blk = window // dilation
    span = blk * dilation
    s_pad = -(-S // span) * span
    pad = ((0, 0), (0, s_pad - S), (0, 0), (0, 0))

    def to_blocks(t):
        t = jnp.pad(t, pad).reshape(B, s_pad // dilation, dilation, H, Dh)
        t = t.transpose(0, 2, 1, 3, 4)
        return t.reshape(B, dilation, -1, blk, H, Dh)

    def with_prev(t):
        prev = jnp.pad(t, ((0, 0), (0, 0), (1, 0), (0, 0), (0, 0), (0, 0)))[:, :, :-1]
        return jnp.concatenate([prev, t], axis=3)

    qb = to_blocks(q)
    kk = with_prev(to_blocks(k))
    vv = with_prev(to_blocks(v))
    nb = qb.shape[2]
    a = jnp.arange(blk)
    b = jnp.arange(2 * blk)
    dist = blk + a[:, None] - b[None, :]
    band = (dist >= 0) & (dist <= blk)
    mask = band[None] & ((jnp.arange(nb)[:, None, None] > 0) | (b >= blk)[None, None, :])
    s = jnp.einsum('brnqhd,brnkhd->brnhqk', qb, kk).astype(jnp.float32) * (Dh ** -0.5)
    s = jnp.where(mask[None, None, :, None], s, NEG)
    m = jnp.max(s, axis=-1, keepdims=True)
    p = jnp.exp(s - m)
    den = jnp.sum(p, axis=-1)
    o = jnp.einsum('brnhqk,brnkhd->brnqhd', (p / den[..., None]).astype(vv.dtype), vv)
    lse = m[..., 0] + jnp.log(den)
    o = o.reshape(B, dilation, -1, H, Dh).transpose(0, 2, 1, 3, 4).reshape(B, s_pad, H, Dh)[:, :S]
    lse = lse.transpose(0, 1, 2, 4, 3).reshape(B, dilation, -1, H).transpose(0, 2, 1, 3).reshape(B, s_pad, H)[:, :S]
    return o, lse


def dilated_sample(q, kk, vv, window, dilation):
    T = q.shape[1]
    L = kk.shape[1] - T
    nk = window // dilation + 1
    idx = L + jnp.arange(T)[:, None] - dilation * jnp.arange(nk)[None, :]
    valid = idx >= 0
    gi = jnp.maximum(idx, 0)
    kg = jnp.take(kk, gi, axis=1)
    vg = jnp.take(vv, gi, axis=1)
    s = jnp.einsum('bthd,btkhd->bthk', q, kg).astype(jnp.float32) * (q.shape[-1] ** -0.5)
    s = jnp.where(valid[None, :, None, :], s, NEG)
    m = jnp.max(s, axis=-1, keepdims=True)
    p = jnp.exp(s - m)
    den = jnp.sum(p, axis=-1)
    o = jnp.einsum('bthk,btkhd->bthd', (p / den[..., None]).astype(vg.dtype), vg)
    lse = m[..., 0] + jnp.log(den)
    return o, lse


def dilated_mixer(xn, k_list, v_list, w_q, w_o, sample):
    B, T, _ = xn.shape
    q = (xn @ w_q).reshape(B, T, N_GROUPS, DIL_HEADS, DIL_HD)
    outs, lses = [], []
    for gi, (win, dil) in enumerate(DIL_PAIRS):
        if sample:
            o, lse = dilated_sample(q[:, :, gi], k_list[gi], v_list[gi], win, dil)
        else:
            o, lse = dilated_prompt(q[:, :, gi], k_list[gi], v_list[gi], win, dil)
        outs.append(o)
        lses.append(lse)
    w = jax.nn.softmax(jnp.stack(lses, axis=0), axis=0)
    o = jnp.sum(w[..., None] * jnp.stack(outs, axis=0).astype(jnp.float32), axis=0)
    return o.reshape(B, T, DIL_HEADS * DIL_HD).astype(xn.dtype) @ w_o


def shared_kv(h, g, w_kv):
    B, T, _ = h.shape
    kv = (rmsnorm(h, g) @ w_kv).reshape(B, T, N_GROUPS, 2, DIL_HEADS, DIL_HD)
    return [kv[:, :, i, 0] for i in range(N_GROUPS)], [kv[:, :, i, 1] for i in range(N_GROUPS)]


def swiglu(x, w1, w3, w2):
    return (jax.nn.silu(x @ w1) * (x @ w3)) @ w2


def setup_inputs(seed: int = 0) -> dict:
    key = jax.random.key(seed)
    k = jax.random.split(key, 22)
    f32 = jnp.float32

    def nrm(kk, shape, scale=1.0):
        return jax.random.normal(kk, shape, f32) * scale

    ret_in = 2 * RET_HEADS * RET_DK + 2 * RET_HEADS * RET_DV
    dil_w = DIL_HEADS * DIL_HD

    def cshape(w):
        return (DEC_BATCH, min(w, PAST_LEN), DIL_HEADS, DIL_HD)

    w0, w1, w2 = DIL_PAIRS[0][0], DIL_PAIRS[1][0], DIL_PAIRS[2][0]
    return {
        'x_prompt': nrm(k[0], (BATCH, SEQ, D_MODEL)),
        'x_sample': nrm(k[1], (DEC_BATCH, DEC_SEQ, D_MODEL)),
        'state_ret': nrm(k[2], (N_A, DEC_BATCH, RET_HEADS, RET_DK, RET_DV), 0.02),
        'cache_k_w128': nrm(k[3], cshape(w0)),
        'cache_v_w128': nrm(k[4], cshape(w0)),
        'cache_k_w512': nrm(k[5], cshape(w1)),
        'cache_v_w512': nrm(k[6], cshape(w1)),
        'cache_k_w2048': nrm(k[7], cshape(w2)),
        'cache_v_w2048': nrm(k[8], cshape(w2)),
        'norm_mix': 1.0 + nrm(k[9], (DEPTH, D_MODEL), 0.02),
        'norm_ffn': 1.0 + nrm(k[10], (DEPTH, D_MODEL), 0.02),
        'ret_w_in': nrm(k[11], (N_A, D_MODEL, ret_in), D_MODEL ** -0.5),
        'ret_gn': 1.0 + nrm(k[12], (N_A, RET_HEADS * RET_DV), 0.02),
        'ret_w_out': nrm(k[13], (N_A, RET_HEADS * RET_DV, D_MODEL), (RET_HEADS * RET_DV) ** -0.5),
        'kv_norm': 1.0 + nrm(k[14], (D_MODEL,), 0.02),
        'w_kv': nrm(k[15], (D_MODEL, N_GROUPS * 2 * dil_w), D_MODEL ** -0.5),
        'dil_w_q': nrm(k[16], (N_B, D_MODEL, N_GROUPS * dil_w), D_MODEL ** -0.5),
        'dil_w_o': nrm(k[17], (N_B, dil_w, D_MODEL), dil_w ** -0.5),
        'ffn_w1': nrm(k[18], (DEPTH, D_MODEL, D_FF), D_MODEL ** -0.5),
        'ffn_w3': nrm(k[19], (DEPTH, D_MODEL, D_FF), D_MODEL ** -0.5),
        'ffn_w2': nrm(k[20], (DEPTH, D_FF, D_MODEL), D_FF ** -0.5),
        'norm_final': 1.0 + nrm(k[21], (D_MODEL,), 0.02),
    }


def reference(x_prompt, x_sample, state_ret, cache_k_w128, cache_v_w128, cache_k_w512, cache_v_w512,
              cache_k_w2048, cache_v_w2048, norm_mix, norm_ffn, ret_w_in, ret_gn, ret_w_out,
              kv_norm, w_kv, dil_w_q, dil_w_o, ffn_w1, ffn_w3, ffn_w2, norm_final):
    h_p, h_s = x_prompt, x_sample
    Bp, Tp, _ = x_prompt.shape
    Bs, Ts, _ = x_sample.shape
    pos_p = jnp.arange(Tp)
    pos_s = PAST_LEN + jnp.arange(Ts)
    chunk_p = RET_CHUNK if Tp % RET_CHUNK == 0 else Tp
    bufs_k = [cache_k_w128, cache_k_w512, cache_k_w2048]
    bufs_v = [cache_v_w128, cache_v_w512, cache_v_w2048]
    ret_p, ret_s = [], []
    kp = vp = ks = vs = None
    new_kp = new_vp = new_ks = new_vs = None
    for layer in range(DEPTH):
        if layer < N_A:
            s0 = jnp.zeros((Bp, RET_HEADS, RET_DK, RET_DV), jnp.float32)
            o_p, sp = retention_mixer(rmsnorm(h_p, norm_mix[layer]), s0, pos_p, chunk_p,
                                      ret_w_in[layer], ret_gn[layer], ret_w_out[layer])
            o_s, ss = retention_mixer(rmsnorm(h_s, norm_mix[layer]), state_ret[layer], pos_s, Ts,
                                      ret_w_in[layer], ret_gn[layer], ret_w_out[layer])
            h_p = h_p + o_p
            h_s = h_s + o_s
            ret_p.append(sp)
            ret_s.append(ss)
        else:
            if layer == N_A:
                kp, vp = shared_kv(h_p, kv_norm, w_kv)
                knew, vnew = shared_kv(h_s, kv_norm, w_kv)
                ks = [jnp.concatenate([bufs_k[i], knew[i].astype(bufs_k[i].dtype)], axis=1) for i in range(N_GROUPS)]
                vs = [jnp.concatenate([bufs_v[i], vnew[i].astype(bufs_v[i].dtype)], axis=1) for i in range(N_GROUPS)]
                new_kp = [kp[i][:, -min(DIL_PAIRS[i][0], Tp):] for i in range(N_GROUPS)]
                new_vp = [vp[i][:, -min(DIL_PAIRS[i][0], Tp):] for i in range(N_GROUPS)]
                new_ks = [ks[i][:, Ts:] for i in range(N_GROUPS)]
                new_vs = [vs[i][:, Ts:] for i in range(N_GROUPS)]
            j = layer - N_A
            h_p = h_p + dilated_mixer(rmsnorm(h_p, norm_mix[layer]), kp, vp, dil_w_q[j], dil_w_o[j], False)
            h_s = h_s + dilated_mixer(rmsnorm(h_s, norm_mix[layer]), ks, vs, dil_w_q[j], dil_w_o[j], True)
        h_p = h_p + swiglu(rmsnorm(h_p, norm_ffn[layer]), ffn_w1[layer], ffn_w3[layer], ffn_w2[layer])
        h_s = h_s + swiglu(rmsnorm(h_s, norm_ffn[layer]), ffn_w1[layer], ffn_w3[layer], ffn_w2[layer])
    y_prompt = rmsnorm(h_p, norm_final)
    y_sample = rmsnorm(h_s, norm_final)
    state_ret_prompt = jnp.stack(ret_p, axis=0)
    state_ret_sample = jnp.stack(ret_s, axis=0)
    return (y_prompt, y_sample, state_ret_prompt, state_ret_sample,
            new_kp[0], new_vp[0], new_kp[1], new_vp[1], new_kp[2], new_vp[2],
            new_ks[0], new_vs[0], new_ks[1], new_vs[1], new_ks[2], new_vs[2])
```

```python
import contextlib
import numpy as np
import ml_dtypes
import concourse.bass as bass
import concourse.mybir as mybir
from concourse.bass_utils import run_bass_kernel_spmd

F32 = mybir.dt.float32
BF16 = mybir.dt.bfloat16
AF = mybir.ActivationFunctionType
ALU = mybir.AluOpType

D = 2048
NCH = 16
DFF = 5632
RH, RDK, RDV = 8, 256, 512
DH, HD = 16, 128
PAST = 8192
EPS = 1e-6
TT = 512
WINS = (128, 512, 2048)
DILS = (1, 4, 16)
NS = 4
NST = 16


class Prog:
    def __init__(self, nc, sems, dma_sem_pool):
        self.nc = nc
        self.q = {e: [] for e in ("pe", "dve", "act", "pool", "sp")}
        self.sem = sems
        self.cnt = {e: 0 for e in self.q}
        self.waited = {e: {} for e in self.q}
        self.res = {}
        self.dpool = list(dma_sem_pool)
        self.dsem = {}
        self.dtot = {}

    def _r(self, name):
        if name not in self.res:
            self.res[name] = [None, []]
        return self.res[name]

    def _wait(self, eng, ev):
        if ev is None:
            return
        key, val = ev
        if key.startswith("dma:"):
            val = self.dtot[key]
        elif key == eng:
            if eng == "pe":
                return
            if val > self.cnt[eng]:
                return
        if self.waited[eng].get(key, 0) >= val:
            return
        self.waited[eng][key] = val
        sem = self.dsem[key] if key.startswith("dma:") else self.sem[key]
        self.q[eng].append(lambda e, s=sem, v=val: e.wait_ge(s, v))

    def _deps(self, eng, reads, writes):
        for r in reads:
            self._wait(eng, self._r(r)[0])
        for w in writes:
            st = self._r(w)
            self._wait(eng, st[0])
            for ev in st[1]:
                self._wait(eng, ev)

    def _commit(self, ev, reads, writes):
        for r in reads:
            self._r(r)[1].append(ev)
        for w in writes:
            st = self._r(w)
            st[0] = ev
            st[1] = []

    def op(self, eng, fn, reads=(), writes=(), sig=True):
        self._deps(eng, reads, writes)
        if sig:
            self.cnt[eng] += 1
            ev = (eng, self.cnt[eng])
            s = self.sem[eng]
            self.q[eng].append(lambda e, f=fn, s=s: f(e).then_inc(s, 1))
        else:
            ev = (eng, self.cnt[eng] + 1)
            self.q[eng].append(lambda e, f=fn: f(e))
        self._commit(ev, reads, writes)

    def dma(self, eng, out, in_, semres, reads=(), writes=()):
        self._deps(eng, reads, writes)
        key = "dma:" + semres
        if key not in self.dsem:
            self.dsem[key] = self.dpool.pop()
            self.dtot[key] = 0
        self.dtot[key] += 16
        s = self.dsem[key]
        self.q[eng].append(lambda e, o=out, i=in_, s=s: e.dma_start(out=o, in_=i).then_inc(s, 16))
        self._commit((key, self.dtot[key]), reads, writes)

    def barrier(self):
        for eng in self.q:
            for other in self.q:
                if other != eng and self.cnt[other] > 0:
                    self._wait(eng, (other, self.cnt[other]))
            for key in self.dsem:
                self._wait(eng, (key, self.dtot[key]))
        self.res = {}

    def final_wait(self, eng="sp"):
        for other in self.q:
            if other != eng and self.cnt[other] > 0:
                self._wait(eng, (other, self.cnt[other]))
        for key in self.dsem:
            self._wait(eng, (key, self.dtot[key]))


def host_consts(T):
    half = 128
    inv = (10000.0 ** (-np.arange(half, dtype=np.float32) / half)).astype(np.float32)
    pos = np.arange(T, dtype=np.float32)
    ang = inv[:, None] * pos[None, :]
    cosp, sinp = np.cos(ang).astype(np.float32), np.sin(ang).astype(np.float32)
    poss = (PAST + np.arange(4)).astype(np.float32)
    angs = inv[:, None] * poss[None, :]
    coss = np.tile(np.cos(angs), (1, NS)).astype(np.float32)
    sins = np.tile(np.sin(angs), (1, NS)).astype(np.float32)
    lg = np.log1p(-np.exp2(-5.0 - np.arange(RH, dtype=np.float64)))
    c = {}
    c["cosp"], c["sinp"], c["coss"], c["sins"] = cosp, sinp, coss, sins

    def dec(C):
        i = np.arange(C, dtype=np.float64)
        dist = i[None, :] - i[:, None]
        dm = np.where(dist >= 0, np.exp(np.maximum(dist, 0)[None] * lg[:, None, None]), 0.0)
        dmT = np.zeros((128, RH, C), np.float32)
        dmT[:C] = dm.transpose(1, 0, 2)
        qd = np.exp((i + 1.0)[None, :] * lg[:, None])
        kd = np.exp((C - 1.0 - i)[None, :] * lg[:, None])
        kdT = np.zeros((128, RH), np.float32)
        kdT[:C] = kd.T
        cd = np.exp(C * lg)
        return dmT, qd.astype(np.float32), kdT, np.tile(cd[None, :], (128, 1)).astype(np.float32)

    dm128, qd128, kd128, cd128 = dec(128)
    dm4, qd4, kd4, cd4 = dec(4)
    c["dm128"], c["kd128"], c["cd128"] = dm128, kd128, cd128
    c["dm4"], c["kd4"], c["cd4"] = dm4, kd4, cd4
    c["qdp"] = np.tile(qd128[None], (128, 1, 1)).astype(np.float32)
    c["qds"] = np.tile(np.tile(qd4, (1, NS))[None], (128, 1, 1)).astype(np.float32)
    j = np.arange(128)
    mown = (j[:, None] <= j[None, :]).astype(np.float32)
    mprev = (j[:, None] >= j[None, :]).astype(np.float32)
    ident = np.eye(128, dtype=np.float32)
    u = np.arange(16)
    same = (u[:, None] // 4) == (u[None, :] // 4)
    msn0 = np.zeros((128, 128), np.float32)
    msn0[:16, :16] = (same & (u[:, None] <= u[None, :])).astype(np.float32)
    c["masks"] = np.stack([mown, mprev, ident, msn0], axis=1).astype(ml_dtypes.bfloat16)
    c["identf"] = ident
    onesel = np.zeros((128, DH, DH), np.float32)
    for h in range(DH):
        onesel[:, h, h] = 1.0
    c["onesel"] = onesel.astype(ml_dtypes.bfloat16)
    sel = np.zeros((128, DH, 128), np.float32)
    for h in range(DH):
        sel[h, h, :] = 1.0
    c["sel"] = sel
    return c


CONST_SPECS = None


def build(NT):
    T = TT * NT
    nc = bass.Bass("TRN2", target_bir_lowering=False)
    cst = host_consts(T)

    def din(name, shape, dt=F32):
        return nc.dram_tensor(name, list(shape), dt, kind="ExternalInput").ap()

    def dout(name, shape, dt=F32):
        return nc.dram_tensor(name, list(shape), dt, kind="ExternalOutput").ap()

    def dint(name, shape, dt):
        return nc.dram_tensor(name, list(shape), dt, kind="Internal").ap()

    xp = din("xp", [T, D]); xs = din("xs", [NST, D]); state = din("state", [NS, RH, RDK, RDV])
    ck = [din(f"ck{g}", [NS, WINS[g], D]) for g in range(3)]
    cv = [din(f"cv{g}", [NS, WINS[g], D]) for g in range(3)]
    nvec = din("nvec", [128, 6, NCH]); gnv = din("gnv", [128, 32])
    w_in = din("w_in", [D, 12288]); w_out = din("w_out", [4096, D]); w_kv = din("w_kv", [D, 12288])
    w_q = din("w_q", [D, 6144]); w_o = din("w_o", [D, D])
    w1 = din("w1", [2, D, DFF]); w3 = din("w3", [2, D, DFF]); w2 = din("w2", [2, DFF, D])
    cd = {k: din("c_" + k, v.shape, BF16 if v.dtype == ml_dtypes.bfloat16 else F32) for k, v in cst.items()}

    WO = [min(w, T) for w in WINS]
    y_p = dout("y_p", [T, D]); y_s = dout("y_s", [NST, D])
    st_p = dout("st_p", [RH, RDK, RDV]); st_s = dout("st_s", [NS, RH, RDK, RDV])
    nk = [dout(f"nk{g}", [WO[g], D]) for g in range(3)]
    nv = [dout(f"nv{g}", [WO[g], D]) for g in range(3)]
    nks = [dout(f"nks{g}", [NS, WINS[g], D]) for g in range(3)]
    nvs = [dout(f"nvs{g}", [NS, WINS[g], D]) for g in range(3)]
    KTd = [dint(f"KTd{g}", [128, DH, T], BF16) for g in range(3)]
    Vd = [dint(f"Vd{g}", [T, D], BF16) for g in range(3)]
    Sd = dint("Sd", [RH, RDK, RDV], F32)
    Vsd = dint("Vsd", [3, NST, D], BF16)

    es = contextlib.ExitStack()
    with es:
        def sb(name, shape, dt):
            return es.enter_context(nc.sbuf_tensor(name, list(shape), dt))

        hT = sb("hT", [128, NCH, TT], F32)
        xn = sb("xn", [128, NCH, TT], BF16)
        wb = [sb(f"wb{i}", [128, 8192], BF16) for i in range(2)]
        ARN = 74 * 1024 // 2
        arena = sb("arena", [128, ARN], BF16)
        Sf = sb("Sf", [128, 2, RDV], F32)
        Sb = sb("Sb", [128, 2, RDV], BF16)
        rstd = sb("rstd", [128, TT], F32)
        tmpa = sb("tmpa", [128, TT], F32)
        tmpb = sb("tmpb", [128, TT], F32)
        sq = [sb(f"sq{i}", [128, TT], BF16) for i in range(2)]
        nvec_t = sb("nvec_t", [128, 6, NCH], F32)
        gn_t = sb("gn_t", [128, 32], F32)
        cosT = sb("cosT", [128, TT], F32); sinT = sb("sinT", [128, TT], F32)
        masks = sb("masks", [128, 4, 128], BF16)
        identf = sb("identf", [128, 128], F32)
        onesel = sb("onesel", [128, DH, DH], BF16)
        sel = sb("sel", [128, DH, 128], F32)
        onesb = sb("onesb", [128, 128], BF16)
        dm128 = sb("dm128", [128, RH, 128], F32); kd128 = sb("kd128", [128, RH], F32); cd128 = sb("cd128", [128, RH], F32)
        dm4 = sb("dm4", [128, RH, 4], F32); kd4 = sb("kd4", [128, RH], F32); cd4 = sb("cd4", [128, RH], F32)
        qdp = sb("qdp", [128, RH, 128], F32); qds = sb("qds", [128, RH, NST], F32)
        epsT = sb("epsT", [128, 1], F32)
        kss = sb("kss", [128, 3, DH, NST], BF16)
        den_acc = sb("den_acc", [DH, TT], F32)
        rden = sb("rden", [DH, TT], F32)
        ps = [es.enter_context(nc.psum_tensor(f"ps{i}", [128, 512], F32)) for i in range(8)]
        esem = {e: es.enter_context(nc.semaphore("s_" + e)) for e in ("pe", "dve", "act", "pool", "sp")}
        dpool = [es.enter_context(nc.semaphore(f"d{i}")) for i in range(40)]
        P = Prog(nc, esem, dpool)
        import os
        _os = os
        KDBG = _os.environ.get("KDBG", "")
        dbg_outs = {}

        def dbg(name, ap, reads, dt=F32):
            if not KDBG or name in dbg_outs:
                return
            shp = list(ap.shape)
            t_ = nc.dram_tensor("dbg_" + name, shp, dt, kind="ExternalOutput").ap()
            dbg_outs[name] = t_
            P.dma("sp", t_, ap, "dbg", reads=reads)

        def av(off, shape, dt):
            n = int(np.prod(shape[1:]))
            esz = 4 if dt == F32 else 2
            a = arena[0:shape[0], off // 2: off // 2 + n * esz // 2]
            if dt == F32:
                a = a.bitcast(F32)
            if len(shape) == 3:
                a = a.rearrange("p (a b) -> p a b", a=shape[1])
            elif len(shape) == 4:
                a = a.rearrange("p (a b c) -> p a b c", a=shape[1], b=shape[2])
            return a

        def psb(i):
            return ps[i][:].bitcast(BF16)

        for t_, k_ in ((nvec_t, nvec), (gn_t, gnv), (masks, cd["masks"]), (identf, cd["identf"]), (onesel, cd["onesel"]),
                       (sel, cd["sel"]), (dm128, cd["dm128"]), (kd128, cd["kd128"]), (cd128, cd["cd128"]),
                       (dm4, cd["dm4"][:, :, 0:4]), (kd4, cd["kd4"]), (cd4, cd["cd4"]), (qdp, cd["qdp"]), (qds, cd["qds"])):
            P.dma("sp", t_[:], k_, "const", writes=["const"])
        P.op("pool", lambda e: e.memset(onesb[:], 1.0), writes=["const2"])
        P.op("pool", lambda e: e.memset(epsT[:], EPS), writes=["const2"])
        for g in range(3):
            for s_ in range(NS):
                P.dma("act", nks[g][s_, 0:WINS[g] - 4, :], ck[g][s_, 4:WINS[g], :], "shift")
                P.dma("act", nvs[g][s_, 0:WINS[g] - 4, :], cv[g][s_, 4:WINS[g], :], "shift")
        P.barrier()

        wrr = [0]

        def wload(parts, tag):
            i = wrr[0] % 2
            wrr[0] += 1
            res = f"wb{i}"
            buf = wb[i]
            for (src, kcn, ncols, width, coff) in parts:
                dst = buf[:, 0:kcn * width].rearrange("p (k n) -> p k n", k=kcn)[:, :, coff:coff + ncols]
                P.dma("pool", dst, src.rearrange("(k p) n -> p k n", p=128), res, writes=[res])
            return buf, res

        def wview(buf, kcn, width):
            return buf[:, 0:kcn * width].rearrange("p (k n) -> p k n", k=kcn)

        def load_tokens(src_rows, ntok):
            nsub = (ntok + 127) // 128
            for s_ in range(nsub):
                m = min(128, ntok - 128 * s_)
                stg = av(0, [128, D], F32) if s_ % 2 == 0 else av(8192, [128, D], F32)
                rn = f"ldst{s_ % 2}"
                P.dma("sp", stg[0:m, :], src_rows[128 * s_:128 * s_ + m, :], rn, writes=[rn])
                for b in range(4):
                    pb = 4 + (b % 2)
                    for j in range(4):
                        c_ = 4 * b + j
                        P.op("pe", lambda e, o=ps[pb][:, 128 * j:128 * j + m], i=stg[0:m, 128 * c_:128 * c_ + 128], idn=identf[0:m, 0:m]:
                             e.transpose(o, i, idn), reads=[rn], writes=[f"ps{pb}"], sig=(j == 3))
                    P.op("act", lambda e, o=hT[:, 4 * b:4 * b + 4, 128 * s_:128 * s_ + m],
                         i=ps[pb][:, :].rearrange("p (j m) -> p j m", j=4)[:, :, 0:m]: e.activation(out=o, in_=i, func=AF.Copy),
                         reads=[f"ps{pb}"], writes=["hT"])

        def rmsnorm(vec_idx, ntok, out_xn, out_res, out_f32=None):
            for c_ in range(NCH):
                s_ = sq[c_ % 2]
                P.op("act", lambda e, o=s_[:, 0:ntok], i=hT[:, c_, 0:ntok]: e.activation(out=o, in_=i, func=AF.Square),
                     reads=["hT"], writes=[f"sq{c_ % 2}"])
                P.op("pe", lambda e, r=s_[:, 0:ntok], st=(c_ == 0), sp_=(c_ == NCH - 1): e.matmul(ps[7][:, 0:ntok], lhsT=onesb[:], rhs=r, start=st, stop=sp_),
                     reads=[f"sq{c_ % 2}"], writes=["ps7"], sig=True)
            P.op("act", lambda e: e.activation(out=tmpa[:, 0:ntok], in_=ps[7][:, 0:ntok], func=AF.Sqrt, scale=1.0 / D, bias=epsT[:]),
                 reads=["ps7"], writes=["tmpa"])
            P.op("dve", lambda e: e.reciprocal(out=rstd[:, 0:ntok], in_=tmpa[:, 0:ntok]), reads=["tmpa"], writes=["rstd"])
            for c_ in range(NCH):
                o = (out_f32 if out_f32 is not None else out_xn)[:, c_, 0:ntok]
                P.op("dve", lambda e, o=o, i=hT[:, c_, 0:ntok], g=nvec_t[:, vec_idx, c_:c_ + 1]:
                     e.scalar_tensor_tensor(out=o, in0=i, scalar=g, in1=rstd[:, 0:ntok], op0=ALU.mult, op1=ALU.mult),
                     reads=["hT", "rstd"], writes=[out_res])

        def proj_fm(wv_, res_w, col0, nchunks, rhs_fn, ntok, evac, banks, rhs_res):
            kcn = wv_.shape[1]
            for j in range(nchunks):
                pb = banks[j % len(banks)]
                for kc in range(kcn):
                    P.op("pe", lambda e, l=wv_[:, kc, col0 + 128 * j: col0 + 128 * j + 128], r=rhs_fn(kc), st=(kc == 0), sp_=(kc == kcn - 1), pb=pb:
                         e.matmul(ps[pb][:, 0:ntok], lhsT=l, rhs=r, start=st, stop=sp_),
                         reads=[res_w] + rhs_res, writes=[f"ps{pb}"], sig=(kc == kcn - 1))
                evac(j, pb)

        def add_resid(oc, pb, ntok):
            P.op("dve", lambda e, o=hT[:, oc, 0:ntok], pb=pb: e.tensor_tensor(out=o, in0=ps[pb][:, 0:ntok], in1=o, op=ALU.add),
                 reads=[f"ps{pb}", "hT"], writes=["hT"])

        def ffn(layer, ntok):
            rmsnorm(2 + layer, ntok, xn, "xn")
            hid = av(0, [128, 44, TT], BF16)
            for b in range(11):
                b1, r1 = wload([(w1[layer][:, 512 * b:512 * b + 512], 16, 512, 512, 0)], "w1")
                b3, r3 = wload([(w3[layer][:, 512 * b:512 * b + 512], 16, 512, 512, 0)], "w3")
                v1, v3 = wview(b1, 16, 512), wview(b3, 16, 512)
                for j in range(4):
                    for kc in range(16):
                        P.op("pe", lambda e, l=v1[:, kc, 128 * j:128 * j + 128], r=xn[:, kc, 0:ntok], st=(kc == 0), sp_=(kc == 15), j=j:
                             e.matmul(ps[0 + (j % 2)][:, 0:ntok], lhsT=l, rhs=r, start=st, stop=sp_), reads=[r1, "xn"], writes=[f"ps{j % 2}"], sig=(kc == 15))
                    for kc in range(16):
                        P.op("pe", lambda e, l=v3[:, kc, 128 * j:128 * j + 128], r=xn[:, kc, 0:ntok], st=(kc == 0), sp_=(kc == 15), j=j:
                             e.matmul(ps[2 + (j % 2)][:, 0:ntok], lhsT=l, rhs=r, start=st, stop=sp_), reads=[r3, "xn"], writes=[f"ps{2 + j % 2}"], sig=(kc == 15))
                    tm = tmpa if j % 2 == 0 else tmpb
                    tn = "tmpa" if j % 2 == 0 else "tmpb"
                    P.op("act", lambda e, o=tm[:, 0:ntok], pb=j % 2: e.activation(out=o, in_=ps[pb][:, 0:ntok], func=AF.Silu),
                         reads=[f"ps{j % 2}"], writes=[tn])
                    P.op("dve", lambda e, o=hid[:, 4 * b + j, 0:ntok], a=tm[:, 0:ntok], pb=2 + j % 2: e.tensor_tensor(out=o, in0=a, in1=ps[pb][:, 0:ntok], op=ALU.mult),
                         reads=[tn, f"ps{2 + j % 2}"], writes=["hid"])
            for oc in range(NCH):
                b2, r2 = wload([(w2[layer][:, 128 * oc:128 * oc + 128], 44, 128, 128, 0)], "w2")
                v2 = wview(b2, 44, 128)
                pb = 4 + oc % 2
                for kc in range(44):
                    P.op("pe", lambda e, l=v2[:, kc, :], r=hid[:, kc, 0:ntok], st=(kc == 0), sp_=(kc == 43), pb=pb:
                         e.matmul(ps[pb][:, 0:ntok], lhsT=l, rhs=r, start=st, stop=sp_), reads=[r2, "hid"], writes=[f"ps{pb}"], sig=(kc == 43))
                add_resid(oc, pb, ntok)

        def retention(ntok, C, nchk, sample, ti):
            rmsnorm(0, ntok, xn, "xn")
            gated = av(0, [128, 32, TT], BF16)
            o0 = 32 * 1024
            qh = av(o0, [128, 2, TT], BF16); kh = av(o0 + 2048, [128, 2, TT], BF16); qdh = av(o0 + 4096, [128, 2, TT], BF16)
            kdh = av(o0 + 6144, [128, 4, 256], BF16); vh = av(o0 + 8192, [128, 4, 512], BF16); gsh = av(o0 + 12288, [128, 4, TT], BF16)
            scm = av(o0 + 16384, [128, 4, 128], BF16); osb = av(o0 + 18432, [128, 4, TT], F32)
            t1 = av(o0 + 26624, [128, TT], F32); t2 = av(o0 + 28672, [128, TT], F32); t3 = av(o0 + 30720, [128, TT], F32)
            ob = av(o0 + 32768, [128, 4, TT], BF16); osq = av(o0 + 36864, [128, 4, TT], BF16)
            dm, kdc, cdc, qdt = (dm4, kd4, cd4, qds) if sample else (dm128, kd128, cd128, qdp)
            for h in range(RH):
                bqk, rqk = wload([(w_in[:, 256 * h:256 * h + 256], 16, 256, 512, 0),
                                  (w_in[:, 2048 + 256 * h:2048 + 256 * h + 256], 16, 256, 512, 256)], "qk")
                bv, rv = wload([(w_in[:, 4096 + 512 * h:4096 + 512 * h + 512], 16, 512, 512, 0)], "v")
                vqk, vv = wview(bqk, 16, 512), wview(bv, 16, 512)
                for j in range(4):
                    for kc in range(16):
                        P.op("pe", lambda e, l=vqk[:, kc, 128 * j:128 * j + 128], r=xn[:, kc, 0:ntok], st=(kc == 0), sp_=(kc == 15), j=j:
                             e.matmul(ps[j][:, 0:ntok], lhsT=l, rhs=r, start=st, stop=sp_), reads=[rqk, "xn"], writes=[f"ps{j}"], sig=(kc == 15))
                bg, rg = wload([(w_in[:, 8192 + 512 * h:8192 + 512 * h + 512], 16, 512, 512, 0)], "g")
                vg = wview(bg, 16, 512)
                for (b0, dst, scl, dn) in ((0, qh, 1.0, "qh"), (2, kh, 1.0 / 16.0, "kh")):
                    P.op("dve", lambda e, b0=b0: e.tensor_tensor(out=t1[:, 0:ntok], in0=ps[b0][:, 0:ntok], in1=cosT[:, 0:ntok], op=ALU.mult), reads=[f"ps{b0}", "rope"], writes=["t1"])
                    P.op("dve", lambda e, b0=b0: e.tensor_tensor(out=t2[:, 0:ntok], in0=ps[b0 + 1][:, 0:ntok], in1=sinT[:, 0:ntok], op=ALU.mult), reads=[f"ps{b0 + 1}", "rope"], writes=["t2"])
                    P.op("dve", lambda e: e.tensor_tensor(out=t3[:, 0:ntok], in0=t1[:, 0:ntok], in1=t2[:, 0:ntok], op=ALU.subtract), reads=["t1", "t2"], writes=["t3"])
                    P.op("act", lambda e, dst=dst, scl=scl: e.activation(out=dst[:, 0, 0:ntok], in_=t3[:, 0:ntok], func=AF.Copy, scale=scl), reads=["t3"], writes=[dn])
                    P.op("dve", lambda e, b0=b0: e.tensor_tensor(out=t1[:, 0:ntok], in0=ps[b0][:, 0:ntok], in1=sinT[:, 0:ntok], op=ALU.mult), reads=[f"ps{b0}", "rope"], writes=["t1"])
                    P.op("dve", lambda e, b0=b0: e.tensor_tensor(out=t2[:, 0:ntok], in0=ps[b0 + 1][:, 0:ntok], in1=cosT[:, 0:ntok], op=ALU.mult), reads=[f"ps{b0 + 1}", "rope"], writes=["t2"])
                    P.op("dve", lambda e: e.tensor_tensor(out=t3[:, 0:ntok], in0=t1[:, 0:ntok], in1=t2[:, 0:ntok], op=ALU.add), reads=["t1", "t2"], writes=["t3"])
                    P.op("act", lambda e, dst=dst, scl=scl: e.activation(out=dst[:, 1, 0:ntok], in_=t3[:, 0:ntok], func=AF.Copy, scale=scl), reads=["t3"], writes=[dn])
                if h == 0 and not sample:
                    dbg("xn0", xn[:, :, :], ["xn"], BF16)
                    dbg("qh", qh[:, :, :], ["qh"], BF16)
                    dbg("kh", kh[:, :, :], ["kh"], BF16)
                for dc in range(2):
                    if sample:
                        P.op("pool", lambda e, dc=dc, h=h: e.tensor_tensor(out=qdh[:, dc, 0:ntok], in0=qh[:, dc, 0:ntok], in1=qdt[:, h, 0:ntok], op=ALU.mult),
                             reads=["qh"], writes=["qdh"])
                    else:
                        P.op("dve", lambda e, dc=dc, h=h: e.tensor_tensor(out=qdh[:, dc, :].rearrange("p (c i) -> p c i", c=4), in0=qh[:, dc, :].rearrange("p (c i) -> p c i", c=4),
                                                                          in1=qdt[:, h, :].unsqueeze(1).broadcast_to([128, 4, 128]), op=ALU.mult),
                             reads=["qh"], writes=["qdh"])
                for c_ in range(nchk):
                    pb = 4 + c_ % 2
                    for kc in range(16):
                        P.op("pe", lambda e, l=xn[:, kc, C * c_:C * c_ + C], r=vv[:, kc, :], st=(kc == 0), sp_=(kc == 15), pb=pb:
                             e.matmul(ps[pb][0:C, :], lhsT=l, rhs=r, start=st, stop=sp_), reads=[rv, "xn"], writes=[f"ps{pb}"], sig=(kc == 15))
                    P.op("act", lambda e, c_=c_, pb=pb: e.activation(out=vh[0:C, c_, :], in_=ps[pb][0:C, :], func=AF.Copy), reads=[f"ps{pb}"], writes=["vh"])
                for j in range(4):
                    pb = 4 + j % 2
                    for kc in range(16):
                        P.op("pe", lambda e, l=vg[:, kc, 128 * j:128 * j + 128], r=xn[:, kc, 0:ntok], st=(kc == 0), sp_=(kc == 15), pb=pb:
                             e.matmul(ps[pb][:, 0:ntok], lhsT=l, rhs=r, start=st, stop=sp_), reads=[rg, "xn"], writes=[f"ps{pb}"], sig=(kc == 15))
                    P.op("act", lambda e, j=j, pb=pb: e.activation(out=gsh[:, j, 0:ntok], in_=ps[pb][:, 0:ntok], func=AF.Silu), reads=[f"ps{pb}"], writes=["gsh"])
                for c_ in range(nchk):
                    for dc in range(2):
                        P.op("pe", lambda e, c_=c_, dc=dc: e.transpose(psb(6)[0:C, 128 * dc:128 * dc + 128], kh[:, dc, C * c_:C * c_ + C], masks[:, 2, :]),
                             reads=["kh"], writes=["ps6"], sig=(dc == 1))
                    P.op("dve", lambda e, c_=c_, h=h: e.tensor_scalar(out=kdh[0:C, c_, :], in0=psb(6)[0:C, 0:256], scalar1=kdc[0:C, h:h + 1], scalar2=None, op0=ALU.mult),
                         reads=["ps6"], writes=["kdh"])
                if not sample:
                    if ti == 0:
                        P.op("pool", lambda e: e.memset(Sf[:], 0.0), writes=["Sf"])
                        P.op("pool", lambda e: e.memset(Sb[:], 0.0), writes=["Sb"])
                    else:
                        P.dma("sp", Sf[:], Sd[h].rearrange("(dc p) e -> p dc e", p=128), "Sf", reads=["Sd"], writes=["Sf"])
                        P.op("act", lambda e: e.activation(out=Sb[:], in_=Sf[:], func=AF.Copy), reads=["Sf"], writes=["Sb"])
                for c_ in range(nchk):
                    tk = slice(C * c_, C * c_ + C)
                    if sample:
                        P.dma("sp", Sf[:], state[c_, h].rearrange("(dc p) e -> p dc e", p=128), "Sf", writes=["Sf"])
                        P.op("act", lambda e: e.activation(out=Sb[:], in_=Sf[:], func=AF.Copy), reads=["Sf"], writes=["Sb"])
                    for dc in range(2):
                        P.op("pe", lambda e, dc=dc, tk=tk: e.matmul(ps[7][0:C, 0:C], lhsT=kh[:, dc, tk], rhs=qh[:, dc, tk], start=(dc == 0), stop=(dc == 1)),
                             reads=["kh", "qh"], writes=["ps7"], sig=(dc == 1))
                    P.op("dve", lambda e, c_=c_, h=h: e.tensor_tensor(out=scm[0:C, c_, 0:C], in0=ps[7][0:C, 0:C], in1=dm[0:C, h, 0:C], op=ALU.mult),
                         reads=["ps7"], writes=["scm"])
                    for ec in range(4):
                        P.op("pe", lambda e, ec=ec, c_=c_, tk=tk: e.matmul(ps[ec][:, tk], lhsT=vh[0:C, c_, 128 * ec:128 * ec + 128], rhs=scm[0:C, c_, 0:C], start=True, stop=False),
                             reads=["vh", "scm"], writes=[f"ps{ec}"], sig=False)
                        for dc in range(2):
                            P.op("pe", lambda e, ec=ec, dc=dc, tk=tk: e.matmul(ps[ec][:, tk], lhsT=Sb[:, dc, 128 * ec:128 * ec + 128], rhs=qdh[:, dc, tk], start=False, stop=(dc == 1)),
                                 reads=["Sb", "qdh"], writes=[f"ps{ec}"], sig=(dc == 1))
                    for dc in range(2):
                        P.op("pe", lambda e, dc=dc, c_=c_: e.matmul(ps[4 + dc][:, :], lhsT=kdh[0:C, c_, 128 * dc:128 * dc + 128], rhs=vh[0:C, c_, :], start=True, stop=True),
                             reads=["kdh", "vh"], writes=[f"ps{4 + dc}"])
                        P.op("dve", lambda e, dc=dc, h=h: e.scalar_tensor_tensor(out=Sf[:, dc, :], in0=Sf[:, dc, :], scalar=cdc[:, h:h + 1], in1=ps[4 + dc][:, :], op0=ALU.mult, op1=ALU.add),
                             reads=[f"ps{4 + dc}", "Sf"], writes=["Sf"])
                    if sample:
                        P.dma("sp", st_s[c_, h].rearrange("(dc p) e -> p dc e", p=128), Sf[:], "Sf", reads=["Sf"])
                    elif c_ < nchk - 1:
                        P.op("act", lambda e: e.activation(out=Sb[:], in_=Sf[:], func=AF.Copy), reads=["Sf"], writes=["Sb"])
                if h == 0 and not sample:
                    dbg("vh", vh[:, :, :], ["vh"], BF16)
                    dbg("kdh", kdh[:, :, :], ["kdh"], BF16)
                    dbg("gsh", gsh[:, :, :], ["gsh"], BF16)
                    dbg("Sf", Sf[:, :, :], ["Sf"])
                if not sample:
                    P.dma("sp", Sd[h].rearrange("(dc p) e -> p dc e", p=128), Sf[:], "Sf", reads=["Sf"], writes=["Sd"])
                    if ti == NT - 1:
                        P.dma("sp", st_p[h].rearrange("(dc p) e -> p dc e", p=128), Sf[:], "Sf", reads=["Sf"])
                for ec in range(4):
                    P.op("act", lambda e, ec=ec: e.activation(out=osb[:, ec, 0:ntok], in_=ps[ec][:, 0:ntok], func=AF.Copy), reads=[f"ps{ec}"], writes=["osb"])
                    P.op("act", lambda e, ec=ec: e.activation(out=ob[:, ec, 0:ntok], in_=ps[ec][:, 0:ntok], func=AF.Copy), reads=[f"ps{ec}"], writes=["ob"])
                    P.op("act", lambda e, ec=ec: e.activation(out=osq[:, ec, 0:ntok], in_=ps[ec][:, 0:ntok], func=AF.Square), reads=[f"ps{ec}"], writes=["osq"])
                for ec in range(4):
                    P.op("pe", lambda e, ec=ec: e.matmul(ps[6][:, 0:ntok], lhsT=onesb[:], rhs=ob[:, ec, 0:ntok], start=(ec == 0), stop=(ec == 3)), reads=["ob"], writes=["ps6"], sig=(ec == 3))
                for ec in range(4):
                    P.op("pe", lambda e, ec=ec: e.matmul(ps[7][:, 0:ntok], lhsT=onesb[:], rhs=osq[:, ec, 0:ntok], start=(ec == 0), stop=(ec == 3)), reads=["osq"], writes=["ps7"], sig=(ec == 3))
                P.op("act", lambda e: e.activation(out=t1[:, 0:ntok], in_=ps[6][:, 0:ntok], func=AF.Copy, scale=1.0 / RDV), reads=["ps6"], writes=["t1"])
                P.op("dve", lambda e: e.tensor_tensor(out=t2[:, 0:ntok], in0=t1[:, 0:ntok], in1=t1[:, 0:ntok], op=ALU.mult), reads=["t1"], writes=["t2"])
                P.op("dve", lambda e: e.scalar_tensor_tensor(out=t3[:, 0:ntok], in0=ps[7][:, 0:ntok], scalar=1.0 / RDV, in1=t2[:, 0:ntok], op0=ALU.mult, op1=ALU.subtract),
                     reads=["ps7", "t2"], writes=["t3"])
                P.op("act", lambda e: e.activation(out=t2[:, 0:ntok], in_=t3[:, 0:ntok], func=AF.Sqrt, bias=epsT[:]), reads=["t3"], writes=["t2"])
                P.op("dve", lambda e: e.reciprocal(out=t3[:, 0:ntok], in_=t2[:, 0:ntok]), reads=["t2"], writes=["t3"])
                for ec in range(4):
                    P.op("dve", lambda e, ec=ec: e.tensor_tensor(out=osb[:, ec, 0:ntok], in0=osb[:, ec, 0:ntok], in1=t1[:, 0:ntok], op=ALU.subtract), reads=["osb", "t1"], writes=["osb"])
                    P.op("dve", lambda e, ec=ec, h=h: e.scalar_tensor_tensor(out=osb[:, ec, 0:ntok], in0=osb[:, ec, 0:ntok], scalar=gn_t[:, 4 * h + ec:4 * h + ec + 1], in1=t3[:, 0:ntok], op0=ALU.mult, op1=ALU.mult),
                         reads=["osb", "t3"], writes=["osb"])
                    P.op("pool", lambda e, ec=ec, h=h: e.tensor_tensor(out=gated[:, 4 * h + ec, 0:ntok], in0=osb[:, ec, 0:ntok], in1=gsh[:, ec, 0:ntok], op=ALU.mult),
                         reads=["osb", "gsh"], writes=["gated"])
            if not sample:
                dbg("gated", gated[:, :, :], ["gated"], BF16)
            for b in range(8):
                bo, ro = wload([(w_out[:, 256 * b:256 * b + 256], 32, 256, 256, 0)], "wo")
                vo = wview(bo, 32, 256)
                for j in range(2):
                    pb = 4 + j
                    for kc in range(32):
                        P.op("pe", lambda e, l=vo[:, kc, 128 * j:128 * j + 128], r=gated[:, kc, 0:ntok], st=(kc == 0), sp_=(kc == 31), pb=pb:
                             e.matmul(ps[pb][:, 0:ntok], lhsT=l, rhs=r, start=st, stop=sp_), reads=[ro, "gated"], writes=[f"ps{pb}"], sig=(kc == 31))
                    add_resid(2 * b + j, pb, ntok)

        def sort_copy(dst, src, ntok, d, res_dst, res_src):
            if d == 1:
                P.op("pool", lambda e: e.tensor_copy(out=dst[:, :, 0:ntok], in_=src[:, :, 0:ntok]), reads=[res_src], writes=[res_dst])
            else:
                for half in range(2):
                    cs = slice(8 * half, 8 * half + 8)
                    P.op("pool", lambda e, cs=cs: e.tensor_copy(out=dst[:, cs, 0:ntok].rearrange("p c (r i) -> p c r i", r=d),
                                                            in_=src[:, cs, 0:ntok].rearrange("p c (i r) -> p c r i", r=d)), reads=[res_src], writes=[res_dst])

        def shared_kv(ntok, sample, ti):
            rmsnorm(4, ntok, xn, "xn")
            kst = av(0, [128, DH, TT], BF16)
            vst = av(16384, [128, 4, D], BF16)
            ost = [av(32768, [128, 512], F32), av(34816, [128, 512], F32)]
            vsn_st = av(36864, [NST, D], BF16)
            nsub = (ntok + 127) // 128
            for g in range(3):
                d = DILS[g]
                for kv in range(2):
                    for hb in range(4):
                        c0 = g * 4096 + kv * 2048 + hb * 512
                        bw, rw = wload([(w_kv[:, c0:c0 + 512], 16, 512, 512, 0)], "kv")
                        vw = wview(bw, 16, 512)
                        dst_o = (nk if kv == 0 else nv)[g]
                        if kv == 0:
                            def evac(j, pb, hb=hb, g=g, d=d):
                                if sample:
                                    P.op("act", lambda e: e.activation(out=kss[:, g, 4 * hb + j, :], in_=ps[pb][:, 0:ntok], func=AF.Copy), reads=[f"ps{pb}"], writes=["kss"])
                                else:
                                    P.op("act", lambda e: e.activation(out=kst[:, 4 * hb + j, :].rearrange("p (r i) -> p r i", r=d),
                                                                       in_=ps[pb][:, 0:ntok].rearrange("p (i r) -> p r i", r=d), func=AF.Copy), reads=[f"ps{pb}"], writes=["kst"])
                            proj_fm(vw, rw, 0, 4, lambda kc: xn[:, kc, 0:ntok], ntok, evac, [0, 1], ["xn"])
                        if sample:
                            for kc in range(16):
                                P.op("pe", lambda e, kc=kc, vw=vw: e.matmul(ps[4][0:NST, :], lhsT=xn[:, kc, 0:NST], rhs=vw[:, kc, :], start=(kc == 0), stop=(kc == 15)),
                                     reads=[rw, "xn"], writes=["ps4"], sig=(kc == 15))
                            if kv == 1:
                                P.op("act", lambda e, g=g, hb=hb: e.activation(out=vsn_st[:, 512 * hb:512 * hb + 512], in_=ps[4][0:NST, :], func=AF.Copy), reads=["ps4"], writes=["vsn_st"])
                                if hb == 3:
                                    P.dma("sp", Vsd[g], vsn_st[:, :], "vsn_st", reads=["vsn_st"], writes=["Vsd"])
                            P.op("act", lambda e: e.activation(out=ost[0][0:NST, :], in_=ps[4][0:NST, :], func=AF.Copy), reads=["ps4"], writes=["ost0"])
                            dso = (nks if kv == 0 else nvs)[g]
                            for s_ in range(NS):
                                P.dma("sp", dso[s_, WINS[g] - 4:WINS[g], 512 * hb:512 * hb + 512], ost[0][4 * s_:4 * s_ + 4, :], "ost0", reads=["ost0"])
                        else:
                            for s_ in range(nsub):
                                t0 = TT * ti + 128 * s_
                                need_out = t0 >= T - WO[g]
                                if kv == 0 and not need_out:
                                    continue
                                pb = 4 + s_ % 2
                                for kc in range(16):
                                    P.op("pe", lambda e, kc=kc, s_=s_, pb=pb, vw=vw: e.matmul(ps[pb][:, :], lhsT=xn[:, kc, 128 * s_:128 * s_ + 128], rhs=vw[:, kc, :], start=(kc == 0), stop=(kc == 15)),
                                         reads=[rw, "xn"], writes=[f"ps{pb}"], sig=(kc == 15))
                                if kv == 1:
                                    P.op("act", lambda e, s_=s_, hb=hb, pb=pb: e.activation(out=vst[:, s_, 512 * hb:512 * hb + 512], in_=ps[pb][:, :], func=AF.Copy), reads=[f"ps{pb}"], writes=["vst"])
                                if need_out:
                                    on = f"ost{s_ % 2}"
                                    P.op("act", lambda e, s_=s_, pb=pb: e.activation(out=ost[s_ % 2][:, :], in_=ps[pb][:, :], func=AF.Copy), reads=[f"ps{pb}"], writes=[on])
                                    r0 = t0 - (T - WO[g])
                                    P.dma("sp", dst_o[r0:r0 + 128, 512 * hb:512 * hb + 512], ost[s_ % 2][:, :], on, reads=[on])
                    if sample:
                        continue
                    if ti == 0 and g < 2:
                        if kv == 0:
                            dbg(f"kst{g}", kst[:, :, :], ["kst"], BF16)
                        else:
                            dbg(f"vst{g}", vst[:, :, :], ["vst"], BF16)
                    n_r = T // d
                    per = TT // d
                    if kv == 0:
                        for h in range(DH):
                            dstv = KTd[g][:, h, :].rearrange("p (r m) -> p r m", r=d)[:, :, per * ti:per * (ti + 1)]
                            P.dma("sp", dstv, kst[:, h, :].rearrange("p (r i) -> p r i", r=d), "kst", reads=["kst"], writes=[f"KTd{g}"])
                    else:
                        pp = 128 // d
                        for s_ in range(nsub):
                            if os.environ.get("KFIXB", ""):
                                t0 = TT * ti + 128 * s_
                                P.dma("sp", Vd[g][t0:t0 + 128, :], vst[:, s_, :], "vst", reads=["vst"], writes=[f"Vd{g}"])
                                continue
                            for r in range(d):
                                sl0 = r * n_r + per * ti + pp * s_
                                P.dma("sp", Vd[g][sl0:sl0 + pp, :], vst[r:128:d, s_, :], "vst", reads=["vst"], writes=[f"Vd{g}"])

        def attention(ntok, sample, ti):
            rmsnorm(1, ntok, xn, "xn")
            qg = av(0, [128, DH, TT], BF16)
            num = av(16384, [128, DH, TT], F32)
            o1 = 49152
            ktl = [av(o1, [128, DH, 128], BF16), av(o1 + 4096, [128, DH, 128], BF16)]
            vtl = [av(o1 + 8192, [128, D], BF16), av(o1 + 12288, [128, D], BF16)]
            pt = [av(o1 + 16384, [128, 512], BF16), av(o1 + 17408, [128, 512], BF16)]
            ktm = vtl[1]
            vsa = av(o1 + 18432, [NST, D], BF16)
            first_acc = [True]
            FIXA = bool(os.environ.get("KFIXA", ""))
            FIXB = bool(os.environ.get("KFIXB", ""))
            if not FIXA:
                P.op("pool", lambda e: e.memset(den_acc[:], 0.0), writes=["den_acc"])
                P.op("pool", lambda e: e.memset(num[:], 0.0), writes=["num"])

            def unit(g, nq, qsl, keytiles, acc_fn, dacc):
                first = (g == 0) and FIXA
                hbn = min(DH, 512 // nq)
                nb = DH // hbn
                nkt = len(keytiles)
                for b in range(nb):
                    ob_ = 2 + b % 2
                    for ki, (kfn, vt, nk_, mk, rl) in enumerate(keytiles):
                        sb_ = ki % 2
                        for hh in range(hbn):
                            h = b * hbn + hh
                            P.op("pe", lambda e, h=h, hh=hh, kfn=kfn, sb_=sb_, nk_=nk_: e.matmul(ps[sb_][0:nk_, hh * nq:(hh + 1) * nq], lhsT=kfn(h), rhs=qg[:, h, qsl], start=True, stop=True),
                                 reads=rl + ["qg"], writes=[f"ps{sb_}"], sig=(hh == hbn - 1))
                        pn = f"pt{sb_}"
                        P.op("act", lambda e, sb_=sb_, nk_=nk_: e.activation(out=pt[sb_][0:nk_, 0:hbn * nq], in_=ps[sb_][0:nk_, 0:hbn * nq], func=AF.Exp), reads=[f"ps{sb_}"], writes=[pn])
                        if mk is not None:
                            P.op("dve", lambda e, sb_=sb_, nk_=nk_, mk=mk: e.tensor_tensor(out=pt[sb_][0:nk_, 0:hbn * nq].rearrange("p (h q) -> p h q", h=hbn),
                                                                                           in0=pt[sb_][0:nk_, 0:hbn * nq].rearrange("p (h q) -> p h q", h=hbn),
                                                                                           in1=mk.unsqueeze(1).broadcast_to([nk_, hbn, nq]), op=ALU.mult), reads=[pn], writes=[pn])
                    for hh in range(hbn):
                        h = b * hbn + hh
                        for ki, (kfn, vt, nk_, mk, rl) in enumerate(keytiles):
                            sb_ = ki % 2
                            P.op("pe", lambda e, h=h, hh=hh, vt=vt, sb_=sb_, nk_=nk_, ki=ki, ob_=ob_: e.matmul(ps[ob_][:, hh * nq:(hh + 1) * nq], lhsT=vt[0:nk_, 128 * h:128 * h + 128], rhs=pt[sb_][0:nk_, hh * nq:(hh + 1) * nq],
                                                                                                         start=(ki == 0), stop=(ki == nkt - 1)),
                                 reads=rl + [f"pt{sb_}"], writes=[f"ps{ob_}"], sig=False)
                        for ki, (kfn, vt, nk_, mk, rl) in enumerate(keytiles):
                            sb_ = ki % 2
                            P.op("pe", lambda e, h=h, hh=hh, sb_=sb_, nk_=nk_, ki=ki, b=b: e.matmul(ps[6][0:DH, 0:nq], lhsT=onesel[0:nk_, h, :], rhs=pt[sb_][0:nk_, hh * nq:(hh + 1) * nq],
                                                                                                 start=(ki == 0 and b == 0 and hh == 0), stop=(ki == nkt - 1 and b == nb - 1 and hh == hbn - 1)),
                                 reads=[f"pt{sb_}"], writes=["ps6"], sig=True)
                    acc = acc_fn(b * hbn, hbn)
                    if first:
                        P.op("dve", lambda e, acc=acc, ob_=ob_: e.tensor_copy(out=acc, in_=ps[ob_][:, 0:hbn * nq].rearrange("p (h q) -> p h q", h=hbn)),
                             reads=[f"ps{ob_}"], writes=["num"])
                    else:
                        P.op("dve", lambda e, acc=acc, ob_=ob_: e.tensor_tensor(out=acc, in0=ps[ob_][:, 0:hbn * nq].rearrange("p (h q) -> p h q", h=hbn), in1=acc, op=ALU.add),
                             reads=[f"ps{ob_}", "num"], writes=["num"])
                if first:
                    P.op("dve", lambda e: e.tensor_copy(out=dacc, in_=ps[6][0:DH, 0:nq]), reads=["ps6"], writes=["den_acc"])
                else:
                    P.op("dve", lambda e: e.tensor_tensor(out=dacc, in0=ps[6][0:DH, 0:nq], in1=dacc, op=ALU.add), reads=["ps6", "den_acc"], writes=["den_acc"])

            for g in range(3):
                d = DILS[g]
                for hb in range(4):
                    c0 = g * 2048 + hb * 512
                    bw, rw = wload([(w_q[:, c0:c0 + 512], 16, 512, 512, 0)], "wq")
                    vw = wview(bw, 16, 512)

                    def evac(j, pb, hb=hb, d=d):
                        dd = 1 if sample else d
                        P.op("act", lambda e: e.activation(out=qg[:, 4 * hb + j, 0:ntok].rearrange("p (r i) -> p r i", r=dd),
                                                           in_=ps[pb][:, 0:ntok].rearrange("p (i r) -> p r i", r=dd), func=AF.Copy, scale=float(HD) ** -0.5), reads=[f"ps{pb}"], writes=["qg"])
                    proj_fm(vw, rw, 0, 4, lambda kc: xn[:, kc, 0:ntok], ntok, evac, [4, 5], ["xn"])
                if not sample:
                    n_r = T // d
                    per = TT // d
                    nq = min(128, per)
                    ucount = TT // nq
                    lrr = [0]

                    def ldtile(r_, ms, nk_):
                        i = lrr[0] % 2
                        lrr[0] += 1
                        slot0 = r_ * n_r + ms
                        if FIXB:
                            vrows = Vd[g][ms:ms + nk_, :] if d == 1 else Vd[g][ms * d + r_:(ms + nk_ - 1) * d + r_ + 1:d, :]
                        else:
                            vrows = Vd[g][slot0:slot0 + nk_, :]
                        for h4 in range(4):
                            P.dma("sp", ktl[i][:, 4 * h4:4 * h4 + 4, 0:nk_], KTd[g][:, 4 * h4:4 * h4 + 4, slot0:slot0 + nk_], f"ktl{i}", reads=[f"KTd{g}"], writes=[f"ktl{i}"])
                        P.dma("sp", vtl[i][0:nk_, :], vrows, f"vtl{i}", reads=[f"Vd{g}"], writes=[f"vtl{i}"])
                        return (lambda h, i=i, nk_=nk_: ktl[i][:, h, 0:nk_]), vtl[i], [f"ktl{i}", f"vtl{i}"]

                    for u in range(ucount):
                        r = (u * nq) // per
                        i0 = (u * nq) % per
                        m0 = per * ti + i0
                        kts = []
                        pm0 = max(0, m0 - 128)
                        if m0 - pm0 > 0:
                            kf, vt, rl = ldtile(r, pm0, m0 - pm0)
                            mk = masks[0:m0 - pm0, 1, 0:nq] if (m0 - pm0) == 128 else None
                            kts.append((kf, vt, m0 - pm0, mk, rl))
                        kf, vt, rl = ldtile(r, m0, nq)
                        kts.append((kf, vt, nq, masks[0:nq, 0, 0:nq], rl))
                        if d == 1:
                            accf = lambda h0, hbn, u=u: num[:, h0:h0 + hbn, u * nq:(u + 1) * nq]
                            dacc = den_acc[:, u * nq:(u + 1) * nq]
                        else:
                            accf = lambda h0, hbn, r=r, i0=i0: num[:, h0:h0 + hbn, :].rearrange("p h (i r) -> p h r i", r=d)[:, :, r, i0:i0 + nq]
                            dacc = den_acc[:, :].rearrange("p (i r) -> p r i", r=d)[:, r, i0:i0 + nq]
                        unit(g, nq, slice(u * nq, (u + 1) * nq), kts, accf, dacc)
                    if ti == 0 and g < 2:
                        dbg(f"num_g{g}", num[:, :, :], ["num"])
                        dbg(f"den_g{g}", den_acc[:, :], ["den_acc"])
                else:
                    W = WINS[g]
                    P.dma("sp", vsa[:, :], Vsd[g], "vsa", reads=["Vsd"], writes=["vsa"])
                    for s_ in range(NS):
                        tl = [None] if g == 0 else list(range(4))
                        for t_ in tl:
                            rows = ck[g][s_, 0:128, :] if g == 0 else ck[g][s_, t_:W:d, :]
                            rowsv = cv[g][s_, 0:128, :] if g == 0 else cv[g][s_, t_:W:d, :]
                            P.dma("pool", ktm[:, :], rows, "vtl1", writes=["vtl1"])
                            P.dma("pool", vtl[0][:, :], rowsv, "vtl0", writes=["vtl0"])
                            for h4 in range(4):
                                for j in range(4):
                                    h = 4 * h4 + j
                                    P.op("pe", lambda e, h=h, j=j: e.transpose(psb(7)[:, 128 * j:128 * j + 128], ktm[:, 128 * h:128 * h + 128], masks[:, 2, :]), reads=["vtl1"], writes=["ps7"], sig=(j == 3))
                                P.op("act", lambda e, h4=h4: e.activation(out=ktl[0][:, 4 * h4:4 * h4 + 4, :], in_=psb(7)[:, 0:512].rearrange("p (j k) -> p j k", j=4), func=AF.Copy), reads=["ps7"], writes=["ktl0"])
                            if g == 0:
                                nq, qsl = 4, slice(4 * s_, 4 * s_ + 4)
                                mk1 = masks[:, 1, 0:4]
                                mk2 = masks[0:NST, 3, 4 * s_:4 * s_ + 4]
                            else:
                                nq, qsl = 1, slice(4 * s_ + t_, 4 * s_ + t_ + 1)
                                mk1 = None
                                mk2 = masks[0:NST, 2, 4 * s_ + t_:4 * s_ + t_ + 1]
                            kts = [((lambda h: ktl[0][:, h, :]), vtl[0], 128, mk1, ["ktl0", "vtl0"]),
                                   ((lambda h, g=g: kss[:, g, h, :]), vsa, NST, mk2, ["kss", "vsa"])]
                            accf = lambda h0, hbn, qsl=qsl: num[:, h0:h0 + hbn, qsl]
                            unit(g, nq, qsl, kts, accf, den_acc[:, qsl])
            if not sample and ti == 0:
                dbg("num", num[:, :, :], ["num"])
                dbg("den", den_acc[:, :], ["den_acc"])
            P.op("dve", lambda e: e.reciprocal(out=rden[:, 0:ntok], in_=den_acc[:, 0:ntok]), reads=["den_acc"], writes=["rden"])
            oT = qg
            for h in range(DH):
                pb = h % 2
                P.op("pe", lambda e, h=h, pb=pb: e.matmul(ps[pb][:, 0:ntok], lhsT=sel[0:DH, h, :], rhs=rden[:, 0:ntok], start=True, stop=True), reads=["rden"], writes=[f"ps{pb}"])
                P.op("dve", lambda e, h=h, pb=pb: e.tensor_tensor(out=oT[:, h, 0:ntok], in0=num[:, h, 0:ntok], in1=ps[pb][:, 0:ntok], op=ALU.mult), reads=["num", f"ps{pb}", "qg"], writes=["oT", "qg"])
            for b in range(4):
                bw, rw = wload([(w_o[:, 512 * b:512 * b + 512], 16, 512, 512, 0)], "wo1")
                vw = wview(bw, 16, 512)
                proj_fm(vw, rw, 0, 4, lambda kc: oT[:, kc, 0:ntok], ntok, lambda j, pb, b=b: add_resid(4 * b + j, pb, ntok), [4, 5], ["oT"])

        def final_out(dst_rows, ntok):
            yf = av(0, [128, NCH, TT], F32)
            rmsnorm(5, ntok, None, "yf", out_f32=yf)
            nsub = (ntok + 127) // 128
            for s_ in range(nsub):
                m = min(128, ntok - 128 * s_)
                stg = av(32768 + 8192 * (s_ % 2), [128, D], F32)
                rn = f"yst{s_ % 2}"
                for b in range(4):
                    pb = 4 + b % 2
                    for j in range(4):
                        c_ = 4 * b + j
                        P.op("pe", lambda e, j=j, c_=c_, pb=pb, m=m, s_=s_: e.transpose(ps[pb][0:m, 128 * j:128 * j + 128], yf[:, c_, 128 * s_:128 * s_ + m], identf[:, :]),
                             reads=["yf"], writes=[f"ps{pb}"], sig=(j == 3))
                    P.op("act", lambda e, b=b, pb=pb, m=m, stg=stg: e.activation(out=stg[0:m, 512 * b:512 * b + 512], in_=ps[pb][0:m, :], func=AF.Copy), reads=[f"ps{pb}"], writes=[rn])
                P.dma("sp", dst_rows[128 * s_:128 * s_ + m, :], stg[0:m, :], rn, reads=[rn])

        import os
        KSTOP = int(os.environ.get("KSTOP", "99"))
        tiles = [("p", ti) for ti in range(NT)] + [("s", 0)]
        if os.environ.get("KTILES"):
            tiles = [t for t in tiles if t[0] in os.environ["KTILES"]]
        for kind, ti in tiles:
            sample = kind == "s"
            ntok = NST if sample else TT
            if sample:
                P.dma("sp", cosT[:, 0:NST], cd["coss"], "rope", writes=["rope"])
                P.dma("sp", sinT[:, 0:NST], cd["sins"], "rope", writes=["rope"])
                load_tokens(xs, NST)
            else:
                P.dma("sp", cosT[:, :], cd["cosp"][:, TT * ti:TT * ti + TT], "rope", writes=["rope"])
                P.dma("sp", sinT[:, :], cd["sinp"][:, TT * ti:TT * ti + TT], "rope", writes=["rope"])
                load_tokens(xp[TT * ti:TT * ti + TT, :], TT)
            if not sample and ti == 0:
                dbg("hT0", hT[:, :, :], ["hT"])
            P.barrier()
            if KSTOP >= 1:
                retention(ntok, 4 if sample else 128, 4, sample, ti)
            if not sample and ti == 0:
                dbg("hT1", hT[:, :, :], ["hT"])
            P.barrier()
            if KSTOP >= 2:
                ffn(0, ntok)
            if not sample and ti == 0:
                dbg("hT2", hT[:, :, :], ["hT"])
            P.barrier()
            if KSTOP >= 3:
                shared_kv(ntok, sample, ti)
            P.barrier()
            if KSTOP >= 4:
                attention(ntok, sample, ti)
            if not sample and ti == 0:
                dbg("hT3", hT[:, :, :], ["hT"])
            if sample:
                dbg("hT3s", hT[:, :, 0:NST], ["hT"])
            P.barrier()
            if KSTOP >= 5:
                ffn(1, ntok)
            P.barrier()
            final_out(y_s if sample else y_p[TT * ti:TT * ti + TT, :], ntok)
            P.barrier()
        P.final_wait("sp")

        with nc.Block() as block:
            @block.tensor
            def _(e):
                for f in P.q["pe"]:
                    f(e)

            @block.vector
            def _(e):
                for f in P.q["dve"]:
                    f(e)

            @block.scalar
            def _(e):
                for f in P.q["act"]:
                    f(e)

            @block.gpsimd
            def _(e):
                for f in P.q["pool"]:
                    f(e)

            @block.sync
            def _(e):
                for f in P.q["sp"]:
                    f(e)
    return nc, cst


_CACHE = {}


def make_inputs(NT, c, inputs, cst):
    T = TT * NT
    f = lambda a: np.ascontiguousarray(np.asarray(a, dtype=np.float32))
    b = c % 2
    m = {}
    m["xp"] = f(inputs["x_prompt"][b][:T])
    m["xs"] = f(inputs["x_sample"][NS * c:NS * c + NS]).reshape(NST, D)
    m["state"] = f(inputs["state_ret"][0][NS * c:NS * c + NS])
    for g, w in enumerate(WINS):
        m[f"ck{g}"] = f(inputs[f"cache_k_w{w}"][NS * c:NS * c + NS]).reshape(NS, w, D)
        m[f"cv{g}"] = f(inputs[f"cache_v_w{w}"][NS * c:NS * c + NS]).reshape(NS, w, D)
    vecs = [inputs["norm_mix"][0], inputs["norm_mix"][1], inputs["norm_ffn"][0], inputs["norm_ffn"][1], inputs["kv_norm"], inputs["norm_final"]]
    m["nvec"] = f(np.stack([np.asarray(v).reshape(NCH, 128).T for v in vecs], axis=1))
    m["gnv"] = f(np.asarray(inputs["ret_gn"][0]).reshape(32, 128).T)
    m["w_in"] = f(inputs["ret_w_in"][0]); m["w_out"] = f(inputs["ret_w_out"][0]); m["w_kv"] = f(inputs["w_kv"])
    m["w_q"] = f(inputs["dil_w_q"][0]); m["w_o"] = f(inputs["dil_w_o"][0])
    m["w1"] = f(inputs["ffn_w1"]); m["w3"] = f(inputs["ffn_w3"]); m["w2"] = f(inputs["ffn_w2"])
    for k, v in cst.items():
        m["c_" + k] = np.ascontiguousarray(v)
    return m


def run(inputs, NT):
    if NT not in _CACHE:
        _CACHE[NT] = build(NT)
    nc, cst = _CACHE[NT]
    T = TT * NT
    import os
    ncores = int(os.environ.get("KCORES", "8"))
    in_maps = [make_inputs(NT, c, inputs, cst) for c in range(ncores)]
    res = run_bass_kernel_spmd(nc, in_maps, core_ids=list(range(ncores)))
    R = list(res.results)
    global LAST_R
    LAST_R = R
    while len(R) < 8:
        R.append(R[len(R) % ncores])
    B = 2
    WO = [min(w, T) for w in WINS]
    y_prompt = np.stack([R[b]["y_p"] for b in range(B)]).astype(np.float32)
    y_sample = np.concatenate([R[c]["y_s"].reshape(NS, 4, D) for c in range(8)]).astype(np.float32)
    st_p = np.stack([R[b]["st_p"] for b in range(B)])[None].astype(np.float32)
    st_s = np.concatenate([R[c]["st_s"] for c in range(8)])[None].astype(np.float32)
    outs = [y_prompt, y_sample, st_p, st_s]
    for g in range(3):
        outs.append(np.stack([R[b][f"nk{g}"].reshape(WO[g], DH, HD) for b in range(B)]).astype(np.float32))
        outs.append(np.stack([R[b][f"nv{g}"].reshape(WO[g], DH, HD) for b in range(B)]).astype(np.float32))
    for g in range(3):
        outs.append(np.concatenate([R[c][f"nks{g}"].reshape(NS, WINS[g], DH, HD) for c in range(8)]).astype(np.float32))
        outs.append(np.concatenate([R[c][f"nvs{g}"].reshape(NS, WINS[g], DH, HD) for c in range(8)]).astype(np.float32))
    return tuple(outs)


def kernel(**inputs):
    NT = np.asarray(inputs["x_prompt"]).shape[1] // TT
    return run(inputs, NT)
```

```python
import contextlib
import numpy as np
import ml_dtypes
import concourse.bass as bass
import concourse.mybir as mybir
from concourse.bass_utils import run_bass_kernel_spmd

F32 = mybir.dt.float32
BF16 = mybir.dt.bfloat16
AF = mybir.ActivationFunctionType
ALU = mybir.AluOpType

D = 2048
NCH = 16
DFF = 5632
RH, RDK, RDV = 8, 256, 512
DH, HD = 16, 128
PAST = 8192
EPS = 1e-6
TT = 512
WINS = (128, 512, 2048)
DILS = (1, 4, 16)
NS = 4
NST = 16


class Prog:
    def __init__(self, nc, sems, dma_sem_pool):
        self.nc = nc
        self.q = {e: [] for e in ("pe", "dve", "act", "pool", "sp")}
        self.sem = sems
        self.cnt = {e: 0 for e in self.q}
        self.waited = {e: {} for e in self.q}
        self.res = {}
        self.dpool = list(dma_sem_pool)
        self.dsem = {}
        self.dtot = {}

    def _r(self, name):
        if name not in self.res:
            self.res[name] = [None, []]
        return self.res[name]

    def _wait(self, eng, ev):
        if ev is None:
            return
        key, val = ev
        if key.startswith("dma:"):
            val = self.dtot[key]
        elif key == eng:
            if eng == "pe":
                return
            if val > self.cnt[eng]:
                return
        if self.waited[eng].get(key, 0) >= val:
            return
        self.waited[eng][key] = val
        sem = self.dsem[key] if key.startswith("dma:") else self.sem[key]
        self.q[eng].append(lambda e, s=sem, v=val: e.wait_ge(s, v))

    def _deps(self, eng, reads, writes):
        for r in reads:
            self._wait(eng, self._r(r)[0])
        for w in writes:
            st = self._r(w)
            self._wait(eng, st[0])
            for ev in st[1]:
                self._wait(eng, ev)

    def _commit(self, ev, reads, writes):
        for r in reads:
            self._r(r)[1].append(ev)
        for w in writes:
            st = self._r(w)
            st[0] = ev
            st[1] = []

    def op(self, eng, fn, reads=(), writes=(), sig=True):
        self._deps(eng, reads, writes)
        if sig:
            self.cnt[eng] += 1
            ev = (eng, self.cnt[eng])
            s = self.sem[eng]
            self.q[eng].append(lambda e, f=fn, s=s: f(e).then_inc(s, 1))
        else:
            ev = (eng, self.cnt[eng] + 1)
            self.q[eng].append(lambda e, f=fn: f(e))
        self._commit(ev, reads, writes)

    def dma(self, eng, out, in_, semres, reads=(), writes=()):
        self._deps(eng, reads, writes)
        key = "dma:" + semres
        if key not in self.dsem:
            self.dsem[key] = self.dpool.pop()
            self.dtot[key] = 0
        self.dtot[key] += 16
        s = self.dsem[key]
        self.q[eng].append(lambda e, o=out, i=in_, s=s: e.dma_start(out=o, in_=i).then_inc(s, 16))
        self._commit((key, self.dtot[key]), reads, writes)

    def barrier(self, full=False):
        for eng in self.q:
            for other in self.q:
                if other != eng and self.cnt[other] > 0:
                    self._wait(eng, (other, self.cnt[other]))
            for key in self.dsem:
                if key.startswith("dma:wc_") and not full:
                    continue
                self._wait(eng, (key, self.dtot[key]))
        self.res = {k: v for k, v in self.res.items() if k.startswith("wc_")}

    def final_wait(self, eng="sp"):
        for other in self.q:
            if other != eng and self.cnt[other] > 0:
                self._wait(eng, (other, self.cnt[other]))
        for key in self.dsem:
            self._wait(eng, (key, self.dtot[key]))


def host_consts(T):
    half = 128
    inv = (10000.0 ** (-np.arange(half, dtype=np.float32) / half)).astype(np.float32)
    pos = np.arange(T, dtype=np.float32)
    ang = inv[:, None] * pos[None, :]
    cosp, sinp = np.cos(ang).astype(np.float32), np.sin(ang).astype(np.float32)
    poss = (PAST + np.arange(4)).astype(np.float32)
    angs = inv[:, None] * poss[None, :]
    coss = np.tile(np.cos(angs), (1, NS)).astype(np.float32)
    sins = np.tile(np.sin(angs), (1, NS)).astype(np.float32)
    lg = np.log1p(-np.exp2(-5.0 - np.arange(RH, dtype=np.float64)))
    c = {}
    c["cosp"], c["sinp"], c["coss"], c["sins"] = cosp, sinp, coss, sins

    def dec(C):
        i = np.arange(C, dtype=np.float64)
        dist = i[None, :] - i[:, None]
        dm = np.where(dist >= 0, np.exp(np.maximum(dist, 0)[None] * lg[:, None, None]), 0.0)
        dmT = np.zeros((128, RH, C), np.float32)
        dmT[:C] = dm.transpose(1, 0, 2)
        qd = np.exp((i + 1.0)[None, :] * lg[:, None])
        kd = np.exp((C - 1.0 - i)[None, :] * lg[:, None])
        kdT = np.zeros((128, RH), np.float32)
        kdT[:C] = kd.T
        cd = np.exp(C * lg)
        return dmT, qd.astype(np.float32), kdT, np.tile(cd[None, :], (128, 1)).astype(np.float32)

    dm128, qd128, kd128, cd128 = dec(128)
    dm4, qd4, kd4, cd4 = dec(4)
    c["dm128"], c["kd128"], c["cd128"] = dm128, kd128, cd128
    c["dm4"], c["kd4"], c["cd4"] = dm4, kd4, cd4
    c["qdp"] = np.tile(qd128[None], (128, 1, 1)).astype(np.float32)
    c["qds"] = np.tile(np.tile(qd4, (1, NS))[None], (128, 1, 1)).astype(np.float32)
    j = np.arange(128)
    mown = (j[:, None] <= j[None, :]).astype(np.float32)
    mprev = (j[:, None] >= j[None, :]).astype(np.float32)
    ident = np.eye(128, dtype=np.float32)
    u = np.arange(16)
    same = (u[:, None] // 4) == (u[None, :] // 4)
    msn0 = np.zeros((128, 128), np.float32)
    msn0[:16, :16] = (same & (u[:, None] <= u[None, :])).astype(np.float32)
    c["masks"] = np.stack([mown, mprev, ident, msn0], axis=1).astype(ml_dtypes.bfloat16)
    c["identf"] = ident
    onesel = np.zeros((128, DH, DH), np.float32)
    for h in range(DH):
        onesel[:, h, h] = 1.0
    c["onesel"] = onesel.astype(ml_dtypes.bfloat16)
    sel = np.zeros((128, DH, 128), np.float32)
    for h in range(DH):
        sel[h, h, :] = 1.0
    c["sel"] = sel
    return c


CONST_SPECS = None


def build(NT):
    T = TT * NT
    nc = bass.Bass("TRN2", target_bir_lowering=False)
    cst = host_consts(T)

    def din(name, shape, dt=F32):
        return nc.dram_tensor(name, list(shape), dt, kind="ExternalInput").ap()

    def dout(name, shape, dt=F32):
        return nc.dram_tensor(name, list(shape), dt, kind="ExternalOutput").ap()

    def dint(name, shape, dt):
        return nc.dram_tensor(name, list(shape), dt, kind="Internal").ap()

    xp = din("xp", [T, D]); xs = din("xs", [NST, D]); state = din("state", [NS, RH, RDK, RDV])
    ck = [din(f"ck{g}", [NS, WINS[g], D]) for g in range(3)]
    cv = [din(f"cv{g}", [NS, WINS[g], D]) for g in range(3)]
    nvec = din("nvec", [128, 6, NCH]); gnv = din("gnv", [128, 32])
    w_in = din("w_in", [D, 12288]); w_out = din("w_out", [4096, D]); w_kv = din("w_kv", [D, 12288])
    w_q = din("w_q", [D, 6144]); w_o = din("w_o", [D, D])
    w1 = din("w1", [2, D, DFF]); w3 = din("w3", [2, D, DFF]); w2 = din("w2", [2, DFF, D])
    cd = {k: din("c_" + k, v.shape, BF16 if v.dtype == ml_dtypes.bfloat16 else F32) for k, v in cst.items()}

    WO = [min(w, T) for w in WINS]
    y_p = dout("y_p", [T, D]); y_s = dout("y_s", [NST, D])
    st_p = dout("st_p", [RH, RDK, RDV]); st_s = dout("st_s", [NS, RH, RDK, RDV])
    nk = [dout(f"nk{g}", [WO[g], D]) for g in range(3)]
    nv = [dout(f"nv{g}", [WO[g], D]) for g in range(3)]
    nks = [dout(f"nks{g}", [NS, WINS[g], D]) for g in range(3)]
    nvs = [dout(f"nvs{g}", [NS, WINS[g], D]) for g in range(3)]
    KTd = [dint(f"KTd{g}", [128, DH, T], BF16) for g in range(3)]
    Vd = [dint(f"Vd{g}", [T, D], BF16) for g in range(3)]
    Sd = dint("Sd", [RH, RDK, RDV], F32)
    Vsd = dint("Vsd", [3, NST, D], BF16)

    es = contextlib.ExitStack()
    with es:
        def sb(name, shape, dt):
            return es.enter_context(nc.sbuf_tensor(name, list(shape), dt))

        hT = sb("hT", [128, NCH, TT], F32)
        xn = sb("xn", [128, NCH, TT], BF16)
        wb = [sb(f"wb{i}", [128, 8192], BF16) for i in range(2)]
        ARN = 74 * 1024 // 2
        arena = sb("arena", [128, ARN], BF16)
        Sf = sb("Sf", [128, 2, RDV], F32)
        Sb = sb("Sb", [128, 2, RDV], BF16)
        rstd = sb("rstd", [128, TT], F32)
        tmpa = sb("tmpa", [128, TT], F32)
        tmpb = sb("tmpb", [128, TT], F32)
        sq = [sb(f"sq{i}", [128, TT], BF16) for i in range(2)]
        nvec_t = sb("nvec_t", [128, 6, NCH], F32)
        gn_t = sb("gn_t", [128, 32], F32)
        cosT = sb("cosT", [128, TT], F32); sinT = sb("sinT", [128, TT], F32)
        masks = sb("masks", [128, 4, 128], BF16)
        identf = sb("identf", [128, 128], F32)
        onesel = sb("onesel", [128, DH, DH], BF16)
        sel = sb("sel", [128, DH, 128], F32)
        onesb = sb("onesb", [128, 128], BF16)
        dm128 = sb("dm128", [128, RH, 128], F32); kd128 = sb("kd128", [128, RH], F32); cd128 = sb("cd128", [128, RH], F32)
        dm4 = sb("dm4", [128, RH, 4], F32); kd4 = sb("kd4", [128, RH], F32); cd4 = sb("cd4", [128, RH], F32)
        qdp = sb("qdp", [128, RH, 128], F32); qds = sb("qds", [128, RH, NST], F32)
        epsT = sb("epsT", [128, 1], F32)
        kss = sb("kss", [128, 3, DH, NST], BF16)
        den_acc = sb("den_acc", [DH, TT], F32)
        rden = sb("rden", [DH, TT], F32)
        ps = [es.enter_context(nc.psum_tensor(f"ps{i}", [128, 512], F32)) for i in range(8)]
        esem = {e: es.enter_context(nc.semaphore("s_" + e)) for e in ("pe", "dve", "act", "pool", "sp")}
        dpool = [es.enter_context(nc.semaphore(f"d{i}")) for i in range(40)]
        P = Prog(nc, esem, dpool)
        import os
        _os = os
        KDBG = _os.environ.get("KDBG", "")
        dbg_outs = {}

        def dbg(name, ap, reads, dt=F32):
            if not KDBG or name in dbg_outs:
                return
            shp = list(ap.shape)
            t_ = nc.dram_tensor("dbg_" + name, shp, dt, kind="ExternalOutput").ap()
            dbg_outs[name] = t_
            P.dma("sp", t_, ap, "dbg", reads=reads)

        def av(off, shape, dt):
            n = int(np.prod(shape[1:]))
            esz = 4 if dt == F32 else 2
            a = arena[0:shape[0], off // 2: off // 2 + n * esz // 2]
            if dt == F32:
                a = a.bitcast(F32)
            if len(shape) == 3:
                a = a.rearrange("p (a b) -> p a b", a=shape[1])
            elif len(shape) == 4:
                a = a.rearrange("p (a b c) -> p a b c", a=shape[1], b=shape[2])
            return a

        def psb(i):
            return ps[i][:].bitcast(BF16)

        for t_, k_ in ((nvec_t, nvec), (gn_t, gnv), (masks, cd["masks"]), (identf, cd["identf"]), (onesel, cd["onesel"]),
                       (sel, cd["sel"]), (dm128, cd["dm128"]), (kd128, cd["kd128"]), (cd128, cd["cd128"]),
                       (dm4, cd["dm4"][:, :, 0:4]), (kd4, cd["kd4"]), (cd4, cd["cd4"]), (qdp, cd["qdp"]), (qds, cd["qds"])):
            P.dma("sp", t_[:], k_, "const", writes=["const"])
        P.op("pool", lambda e: e.memset(onesb[:], 1.0), writes=["const2"])
        P.op("pool", lambda e: e.memset(epsT[:], EPS), writes=["const2"])
        for g in range(3):
            for s_ in range(NS):
                P.dma("act", nks[g][s_, 0:WINS[g] - 4, :], ck[g][s_, 4:WINS[g], :], "shift")
                P.dma("act", nvs[g][s_, 0:WINS[g] - 4, :], cv[g][s_, 4:WINS[g], :], "shift")
        def wcast(name, src, rows_per):
            K_, N_ = src.shape
            dst = dint("b_" + name, [K_, N_], BF16)
            for r0 in range(0, K_, 128):
                for c0 in range(0, N_, 8192):
                    ncol = min(8192, N_ - c0)
                    i = wcr[0] % 2
                    wcr[0] += 1
                    P.dma("pool", wb[i][:, 0:ncol], src[r0:r0 + 128, c0:c0 + ncol], f"wb{i}", writes=[f"wb{i}"])
                    P.dma("sp", dst[r0:r0 + 128, c0:c0 + ncol], wb[i][:, 0:ncol], "wc_" + name, reads=[f"wb{i}"], writes=["wc_" + name])
            return dst

        wcr = [0]
        w_in = wcast("w_in", w_in, 256)
        w_out = wcast("w_out", w_out, 1024)
        w1b = [None, None]; w3b = [None, None]; w2b = [None, None]
        w1b[0] = wcast("w1_0", w1[0], 512); w3b[0] = wcast("w3_0", w3[0], 512); w2b[0] = wcast("w2_0", w2[0], 1408)
        w_kv = wcast("w_kv", w_kv, 256)
        w_q = wcast("w_q", w_q, 512)
        w_o = wcast("w_o", w_o, 1024)
        w1b[1] = wcast("w1_1", w1[1], 512); w3b[1] = wcast("w3_1", w3[1], 512); w2b[1] = wcast("w2_1", w2[1], 1408)
        w1, w3, w2 = w1b, w3b, w2b
        P.barrier(full=True)

        wrr = [0]

        def wload(parts, tag):
            i = wrr[0] % 2
            wrr[0] += 1
            res = f"wb{i}"
            buf = wb[i]
            for (src, kcn, ncols, width, coff) in parts:
                dst = buf[:, 0:kcn * width].rearrange("p (k n) -> p k n", k=kcn)[:, :, coff:coff + ncols]
                P.dma("pool", dst, src.rearrange("(k p) n -> p k n", p=128), res, reads=["wc_" + tag], writes=[res])
            return buf, res

        def wview(buf, kcn, width):
            return buf[:, 0:kcn * width].rearrange("p (k n) -> p k n", k=kcn)

        def load_tokens(src_rows, ntok):
            nsub = (ntok + 127) // 128
            for s_ in range(nsub):
                m = min(128, ntok - 128 * s_)
                stg = av(0, [128, D], F32) if s_ % 2 == 0 else av(8192, [128, D], F32)
                rn = f"ldst{s_ % 2}"
                P.dma("sp", stg[0:m, :], src_rows[128 * s_:128 * s_ + m, :], rn, writes=[rn])
                for b in range(4):
                    pb = 4 + (b % 2)
                    for j in range(4):
                        c_ = 4 * b + j
                        P.op("pe", lambda e, o=ps[pb][:, 128 * j:128 * j + m], i=stg[0:m, 128 * c_:128 * c_ + 128], idn=identf[0:m, 0:m]:
                             e.transpose(o, i, idn), reads=[rn], writes=[f"ps{pb}"], sig=(j == 3))
                    P.op("act", lambda e, o=hT[:, 4 * b:4 * b + 4, 128 * s_:128 * s_ + m],
                         i=ps[pb][:, :].rearrange("p (j m) -> p j m", j=4)[:, :, 0:m]: e.activation(out=o, in_=i, func=AF.Copy),
                         reads=[f"ps{pb}"], writes=["hT"])

        def rmsnorm(vec_idx, ntok, out_xn, out_res, out_f32=None):
            for c_ in range(NCH):
                s_ = sq[c_ % 2]
                P.op("act", lambda e, o=s_[:, 0:ntok], i=hT[:, c_, 0:ntok]: e.activation(out=o, in_=i, func=AF.Square),
                     reads=["hT"], writes=[f"sq{c_ % 2}"])
                P.op("pe", lambda e, r=s_[:, 0:ntok], st=(c_ == 0), sp_=(c_ == NCH - 1): e.matmul(ps[7][:, 0:ntok], lhsT=onesb[:], rhs=r, start=st, stop=sp_),
                     reads=[f"sq{c_ % 2}"], writes=["ps7"], sig=True)
            P.op("act", lambda e: e.activation(out=tmpa[:, 0:ntok], in_=ps[7][:, 0:ntok], func=AF.Sqrt, scale=1.0 / D, bias=epsT[:]),
                 reads=["ps7"], writes=["tmpa"])
            P.op("dve", lambda e: e.reciprocal(out=rstd[:, 0:ntok], in_=tmpa[:, 0:ntok]), reads=["tmpa"], writes=["rstd"])
            for c_ in range(NCH):
                o = (out_f32 if out_f32 is not None else out_xn)[:, c_, 0:ntok]
                P.op("dve", lambda e, o=o, i=hT[:, c_, 0:ntok], g=nvec_t[:, vec_idx, c_:c_ + 1]:
                     e.scalar_tensor_tensor(out=o, in0=i, scalar=g, in1=rstd[:, 0:ntok], op0=ALU.mult, op1=ALU.mult),
                     reads=["hT", "rstd"], writes=[out_res])

        def proj_fm(wv_, res_w, col0, nchunks, rhs_fn, ntok, evac, banks, rhs_res):
            kcn = wv_.shape[1]
            for j in range(nchunks):
                pb = banks[j % len(banks)]
                for kc in range(kcn):
                    P.op("pe", lambda e, l=wv_[:, kc, col0 + 128 * j: col0 + 128 * j + 128], r=rhs_fn(kc), st=(kc == 0), sp_=(kc == kcn - 1), pb=pb:
                         e.matmul(ps[pb][:, 0:ntok], lhsT=l, rhs=r, start=st, stop=sp_),
                         reads=[res_w] + rhs_res, writes=[f"ps{pb}"], sig=(kc == kcn - 1))
                evac(j, pb)

        def add_resid(oc, pb, ntok):
            P.op("dve", lambda e, o=hT[:, oc, 0:ntok], pb=pb: e.tensor_tensor(out=o, in0=ps[pb][:, 0:ntok], in1=o, op=ALU.add),
                 reads=[f"ps{pb}", "hT"], writes=["hT"])

        def ffn(layer, ntok):
            rmsnorm(2 + layer, ntok, xn, "xn")
            hid = av(0, [128, 44, TT], BF16)
            for b in range(11):
                b1, r1 = wload([(w1[layer][:, 512 * b:512 * b + 512], 16, 512, 512, 0)], f"w1_{layer}")
                b3, r3 = wload([(w3[layer][:, 512 * b:512 * b + 512], 16, 512, 512, 0)], f"w3_{layer}")
                v1, v3 = wview(b1, 16, 512), wview(b3, 16, 512)
                for j in range(4):
                    for kc in range(16):
                        P.op("pe", lambda e, l=v1[:, kc, 128 * j:128 * j + 128], r=xn[:, kc, 0:ntok], st=(kc == 0), sp_=(kc == 15), j=j:
                             e.matmul(ps[0 + (j % 2)][:, 0:ntok], lhsT=l, rhs=r, start=st, stop=sp_), reads=[r1, "xn"], writes=[f"ps{j % 2}"], sig=(kc == 15))
                    for kc in range(16):
                        P.op("pe", lambda e, l=v3[:, kc, 128 * j:128 * j + 128], r=xn[:, kc, 0:ntok], st=(kc == 0), sp_=(kc == 15), j=j:
                             e.matmul(ps[2 + (j % 2)][:, 0:ntok], lhsT=l, rhs=r, start=st, stop=sp_), reads=[r3, "xn"], writes=[f"ps{2 + j % 2}"], sig=(kc == 15))
                    tm = tmpa if j % 2 == 0 else tmpb
                    tn = "tmpa" if j % 2 == 0 else "tmpb"
                    P.op("act", lambda e, o=tm[:, 0:ntok], pb=j % 2: e.activation(out=o, in_=ps[pb][:, 0:ntok], func=AF.Silu),
                         reads=[f"ps{j % 2}"], writes=[tn])
                    P.op("dve", lambda e, o=hid[:, 4 * b + j, 0:ntok], a=tm[:, 0:ntok], pb=2 + j % 2: e.tensor_tensor(out=o, in0=a, in1=ps[pb][:, 0:ntok], op=ALU.mult),
                         reads=[tn, f"ps{2 + j % 2}"], writes=["hid"])
            for oc in range(NCH):
                b2, r2 = wload([(w2[layer][:, 128 * oc:128 * oc + 128], 44, 128, 128, 0)], f"w2_{layer}")
                v2 = wview(b2, 44, 128)
                pb = 4 + oc % 2
                for kc in range(44):
                    P.op("pe", lambda e, l=v2[:, kc, :], r=hid[:, kc, 0:ntok], st=(kc == 0), sp_=(kc == 43), pb=pb:
                         e.matmul(ps[pb][:, 0:ntok], lhsT=l, rhs=r, start=st, stop=sp_), reads=[r2, "hid"], writes=[f"ps{pb}"], sig=(kc == 43))
                add_resid(oc, pb, ntok)

        def retention(ntok, C, nchk, sample, ti):
            rmsnorm(0, ntok, xn, "xn")
            gated = av(0, [128, 32, TT], BF16)
            o0 = 32 * 1024
            qh = av(o0, [128, 2, TT], BF16); kh = av(o0 + 2048, [128, 2, TT], BF16); qdh = av(o0 + 4096, [128, 2, TT], BF16)
            kdh = av(o0 + 6144, [128, 4, 256], BF16); vh = av(o0 + 8192, [128, 4, 512], BF16); gsh = av(o0 + 12288, [128, 4, TT], BF16)
            scm = av(o0 + 16384, [128, 4, 128], BF16); osb = av(o0 + 18432, [128, 4, TT], F32)
            t1 = av(o0 + 26624, [128, TT], F32); t2 = av(o0 + 28672, [128, TT], F32); t3 = av(o0 + 30720, [128, TT], F32)
            ob = av(o0 + 32768, [128, 4, TT], BF16); osq = av(o0 + 36864, [128, 4, TT], BF16)
            dm, kdc, cdc, qdt = (dm4, kd4, cd4, qds) if sample else (dm128, kd128, cd128, qdp)
            for h in range(RH):
                bqk, rqk = wload([(w_in[:, 256 * h:256 * h + 256], 16, 256, 512, 0),
                                  (w_in[:, 2048 + 256 * h:2048 + 256 * h + 256], 16, 256, 512, 256)], "w_in")
                bv, rv = wload([(w_in[:, 4096 + 512 * h:4096 + 512 * h + 512], 16, 512, 512, 0)], "w_in")
                vqk, vv = wview(bqk, 16, 512), wview(bv, 16, 512)
                for j in range(4):
                    for kc in range(16):
                        P.op("pe", lambda e, l=vqk[:, kc, 128 * j:128 * j + 128], r=xn[:, kc, 0:ntok], st=(kc == 0), sp_=(kc == 15), j=j:
                             e.matmul(ps[j][:, 0:ntok], lhsT=l, rhs=r, start=st, stop=sp_), reads=[rqk, "xn"], writes=[f"ps{j}"], sig=(kc == 15))
                bg, rg = wload([(w_in[:, 8192 + 512 * h:8192 + 512 * h + 512], 16, 512, 512, 0)], "w_in")
                vg = wview(bg, 16, 512)
                for (b0, dst, scl, dn) in ((0, qh, 1.0, "qh"), (2, kh, 1.0 / 16.0, "kh")):
                    P.op("dve", lambda e, b0=b0: e.tensor_tensor(out=t1[:, 0:ntok], in0=ps[b0][:, 0:ntok], in1=cosT[:, 0:ntok], op=ALU.mult), reads=[f"ps{b0}", "rope"], writes=["t1"])
                    P.op("dve", lambda e, b0=b0: e.tensor_tensor(out=t2[:, 0:ntok], in0=ps[b0 + 1][:, 0:ntok], in1=sinT[:, 0:ntok], op=ALU.mult), reads=[f"ps{b0 + 1}", "rope"], writes=["t2"])
                    P.op("dve", lambda e: e.tensor_tensor(out=t3[:, 0:ntok], in0=t1[:, 0:ntok], in1=t2[:, 0:ntok], op=ALU.subtract), reads=["t1", "t2"], writes=["t3"])
                    P.op("act", lambda e, dst=dst, scl=scl: e.activation(out=dst[:, 0, 0:ntok], in_=t3[:, 0:ntok], func=AF.Copy, scale=scl), reads=["t3"], writes=[dn])
                    P.op("dve", lambda e, b0=b0: e.tensor_tensor(out=t1[:, 0:ntok], in0=ps[b0][:, 0:ntok], in1=sinT[:, 0:ntok], op=ALU.mult), reads=[f"ps{b0}", "rope"], writes=["t1"])
                    P.op("dve", lambda e, b0=b0: e.tensor_tensor(out=t2[:, 0:ntok], in0=ps[b0 + 1][:, 0:ntok], in1=cosT[:, 0:ntok], op=ALU.mult), reads=[f"ps{b0 + 1}", "rope"], writes=["t2"])
                    P.op("dve", lambda e: e.tensor_tensor(out=t3[:, 0:ntok], in0=t1[:, 0:ntok], in1=t2[:, 0:ntok], op=ALU.add), reads=["t1", "t2"], writes=["t3"])
                    P.op("act", lambda e, dst=dst, scl=scl: e.activation(out=dst[:, 1, 0:ntok], in_=t3[:, 0:ntok], func=AF.Copy, scale=scl), reads=["t3"], writes=[dn])
                if h == 0 and not sample:
                    dbg("xn0", xn[:, :, :], ["xn"], BF16)
                    dbg("qh", qh[:, :, :], ["qh"], BF16)
                    dbg("kh", kh[:, :, :], ["kh"], BF16)
                for dc in range(2):
                    if sample:
                        P.op("pool", lambda e, dc=dc, h=h: e.tensor_tensor(out=qdh[:, dc, 0:ntok], in0=qh[:, dc, 0:ntok], in1=qdt[:, h, 0:ntok], op=ALU.mult),
                             reads=["qh"], writes=["qdh"])
                    else:
                        P.op("dve", lambda e, dc=dc, h=h: e.tensor_tensor(out=qdh[:, dc, :].rearrange("p (c i) -> p c i", c=4), in0=qh[:, dc, :].rearrange("p (c i) -> p c i", c=4),
                                                                          in1=qdt[:, h, :].unsqueeze(1).broadcast_to([128, 4, 128]), op=ALU.mult),
                             reads=["qh"], writes=["qdh"])
                for c_ in range(nchk):
                    pb = 4 + c_ % 2
                    for kc in range(16):
                        P.op("pe", lambda e, l=xn[:, kc, C * c_:C * c_ + C], r=vv[:, kc, :], st=(kc == 0), sp_=(kc == 15), pb=pb:
                             e.matmul(ps[pb][0:C, :], lhsT=l, rhs=r, start=st, stop=sp_), reads=[rv, "xn"], writes=[f"ps{pb}"], sig=(kc == 15))
                    P.op("act", lambda e, c_=c_, pb=pb: e.activation(out=vh[0:C, c_, :], in_=ps[pb][0:C, :], func=AF.Copy), reads=[f"ps{pb}"], writes=["vh"])
                for j in range(4):
                    pb = 4 + j % 2
                    for kc in range(16):
                        P.op("pe", lambda e, l=vg[:, kc, 128 * j:128 * j + 128], r=xn[:, kc, 0:ntok], st=(kc == 0), sp_=(kc == 15), pb=pb:
                             e.matmul(ps[pb][:, 0:ntok], lhsT=l, rhs=r, start=st, stop=sp_), reads=[rg, "xn"], writes=[f"ps{pb}"], sig=(kc == 15))
                    P.op("act", lambda e, j=j, pb=pb: e.activation(out=gsh[:, j, 0:ntok], in_=ps[pb][:, 0:ntok], func=AF.Silu), reads=[f"ps{pb}"], writes=["gsh"])
                for c_ in range(nchk):
                    for dc in range(2):
                        P.op("pe", lambda e, c_=c_, dc=dc: e.transpose(psb(6)[0:C, 128 * dc:128 * dc + 128], kh[:, dc, C * c_:C * c_ + C], masks[:, 2, :]),
                             reads=["kh"], writes=["ps6"], sig=(dc == 1))
                    P.op("dve", lambda e, c_=c_, h=h: e.tensor_scalar(out=kdh[0:C, c_, :], in0=psb(6)[0:C, 0:256], scalar1=kdc[0:C, h:h + 1], scalar2=None, op0=ALU.mult),
                         reads=["ps6"], writes=["kdh"])
                if not sample:
                    if ti == 0:
                        P.op("pool", lambda e: e.memset(Sf[:], 0.0), writes=["Sf"])
                        P.op("pool", lambda e: e.memset(Sb[:], 0.0), writes=["Sb"])
                    else:
                        P.dma("sp", Sf[:], Sd[h].rearrange("(dc p) e -> p dc e", p=128), "Sf", reads=["Sd"], writes=["Sf"])
                        P.op("act", lambda e: e.activation(out=Sb[:], in_=Sf[:], func=AF.Copy), reads=["Sf"], writes=["Sb"])
                for c_ in range(nchk):
                    tk = slice(C * c_, C * c_ + C)
                    if sample:
                        P.dma("sp", Sf[:], state[c_, h].rearrange("(dc p) e -> p dc e", p=128), "Sf", writes=["Sf"])
                        P.op("act", lambda e: e.activation(out=Sb[:], in_=Sf[:], func=AF.Copy), reads=["Sf"], writes=["Sb"])
                    for dc in range(2):
                        P.op("pe", lambda e, dc=dc, tk=tk: e.matmul(ps[7][0:C, 0:C], lhsT=kh[:, dc, tk], rhs=qh[:, dc, tk], start=(dc == 0), stop=(dc == 1)),
                             reads=["kh", "qh"], writes=["ps7"], sig=(dc == 1))
                    P.op("dve", lambda e, c_=c_, h=h: e.tensor_tensor(out=scm[0:C, c_, 0:C], in0=ps[7][0:C, 0:C], in1=dm[0:C, h, 0:C], op=ALU.mult),
                         reads=["ps7"], writes=["scm"])
                    for ec in range(4):
                        P.op("pe", lambda e, ec=ec, c_=c_, tk=tk: e.matmul(ps[ec][:, tk], lhsT=vh[0:C, c_, 128 * ec:128 * ec + 128], rhs=scm[0:C, c_, 0:C], start=True, stop=False),
                             reads=["vh", "scm"], writes=[f"ps{ec}"], sig=False)
                        for dc in range(2):
                            P.op("pe", lambda e, ec=ec, dc=dc, tk=tk: e.matmul(ps[ec][:, tk], lhsT=Sb[:, dc, 128 * ec:128 * ec + 128], rhs=qdh[:, dc, tk], start=False, stop=(dc == 1)),
                                 reads=["Sb", "qdh"], writes=[f"ps{ec}"], sig=(dc == 1))
                    for dc in range(2):
                        P.op("pe", lambda e, dc=dc, c_=c_: e.matmul(ps[4 + dc][:, :], lhsT=kdh[0:C, c_, 128 * dc:128 * dc + 128], rhs=vh[0:C, c_, :], start=True, stop=True),
                             reads=["kdh", "vh"], writes=[f"ps{4 + dc}"])
                        P.op("dve", lambda e, dc=dc, h=h: e.scalar_tensor_tensor(out=Sf[:, dc, :], in0=Sf[:, dc, :], scalar=cdc[:, h:h + 1], in1=ps[4 + dc][:, :], op0=ALU.mult, op1=ALU.add),
                             reads=[f"ps{4 + dc}", "Sf"], writes=["Sf"])
                    if sample:
                        P.dma("sp", st_s[c_, h].rearrange("(dc p) e -> p dc e", p=128), Sf[:], "Sf", reads=["Sf"])
                    elif c_ < nchk - 1:
                        P.op("act", lambda e: e.activation(out=Sb[:], in_=Sf[:], func=AF.Copy), reads=["Sf"], writes=["Sb"])
                if h == 0 and not sample:
                    dbg("vh", vh[:, :, :], ["vh"], BF16)
                    dbg("kdh", kdh[:, :, :], ["kdh"], BF16)
                    dbg("gsh", gsh[:, :, :], ["gsh"], BF16)
                    dbg("Sf", Sf[:, :, :], ["Sf"])
                if not sample:
                    P.dma("sp", Sd[h].rearrange("(dc p) e -> p dc e", p=128), Sf[:], "Sf", reads=["Sf"], writes=["Sd"])
                    if ti == NT - 1:
                        P.dma("sp", st_p[h].rearrange("(dc p) e -> p dc e", p=128), Sf[:], "Sf", reads=["Sf"])
                for ec in range(4):
                    P.op("act", lambda e, ec=ec: e.activation(out=osb[:, ec, 0:ntok], in_=ps[ec][:, 0:ntok], func=AF.Copy), reads=[f"ps{ec}"], writes=["osb"])
                    P.op("act", lambda e, ec=ec: e.activation(out=ob[:, ec, 0:ntok], in_=ps[ec][:, 0:ntok], func=AF.Copy), reads=[f"ps{ec}"], writes=["ob"])
                    P.op("act", lambda e, ec=ec: e.activation(out=osq[:, ec, 0:ntok], in_=ps[ec][:, 0:ntok], func=AF.Square), reads=[f"ps{ec}"], writes=["osq"])
                for ec in range(4):
                    P.op("pe", lambda e, ec=ec: e.matmul(ps[6][:, 0:ntok], lhsT=onesb[:], rhs=ob[:, ec, 0:ntok], start=(ec == 0), stop=(ec == 3)), reads=["ob"], writes=["ps6"], sig=(ec == 3))
                for ec in range(4):
                    P.op("pe", lambda e, ec=ec: e.matmul(ps[7][:, 0:ntok], lhsT=onesb[:], rhs=osq[:, ec, 0:ntok], start=(ec == 0), stop=(ec == 3)), reads=["osq"], writes=["ps7"], sig=(ec == 3))
                P.op("act", lambda e: e.activation(out=t1[:, 0:ntok], in_=ps[6][:, 0:ntok], func=AF.Copy, scale=1.0 / RDV), reads=["ps6"], writes=["t1"])
                P.op("dve", lambda e: e.tensor_tensor(out=t2[:, 0:ntok], in0=t1[:, 0:ntok], in1=t1[:, 0:ntok], op=ALU.mult), reads=["t1"], writes=["t2"])
                P.op("dve", lambda e: e.scalar_tensor_tensor(out=t3[:, 0:ntok], in0=ps[7][:, 0:ntok], scalar=1.0 / RDV, in1=t2[:, 0:ntok], op0=ALU.mult, op1=ALU.subtract),
                     reads=["ps7", "t2"], writes=["t3"])
                P.op("act", lambda e: e.activation(out=t2[:, 0:ntok], in_=t3[:, 0:ntok], func=AF.Sqrt, bias=epsT[:]), reads=["t3"], writes=["t2"])
                P.op("dve", lambda e: e.reciprocal(out=t3[:, 0:ntok], in_=t2[:, 0:ntok]), reads=["t2"], writes=["t3"])
                for ec in range(4):
                    P.op("dve", lambda e, ec=ec: e.tensor_tensor(out=osb[:, ec, 0:ntok], in0=osb[:, ec, 0:ntok], in1=t1[:, 0:ntok], op=ALU.subtract), reads=["osb", "t1"], writes=["osb"])
                    P.op("dve", lambda e, ec=ec, h=h: e.scalar_tensor_tensor(out=osb[:, ec, 0:ntok], in0=osb[:, ec, 0:ntok], scalar=gn_t[:, 4 * h + ec:4 * h + ec + 1], in1=t3[:, 0:ntok], op0=ALU.mult, op1=ALU.mult),
                         reads=["osb", "t3"], writes=["osb"])
                    P.op("pool", lambda e, ec=ec, h=h: e.tensor_tensor(out=gated[:, 4 * h + ec, 0:ntok], in0=osb[:, ec, 0:ntok], in1=gsh[:, ec, 0:ntok], op=ALU.mult),
                         reads=["osb", "gsh"], writes=["gated"])
            if not sample:
                dbg("gated", gated[:, :, :], ["gated"], BF16)
            for b in range(8):
                bo, ro = wload([(w_out[:, 256 * b:256 * b + 256], 32, 256, 256, 0)], "w_out")
                vo = wview(bo, 32, 256)
                for j in range(2):
                    pb = 4 + j
                    for kc in range(32):
                        P.op("pe", lambda e, l=vo[:, kc, 128 * j:128 * j + 128], r=gated[:, kc, 0:ntok], st=(kc == 0), sp_=(kc == 31), pb=pb:
                             e.matmul(ps[pb][:, 0:ntok], lhsT=l, rhs=r, start=st, stop=sp_), reads=[ro, "gated"], writes=[f"ps{pb}"], sig=(kc == 31))
                    add_resid(2 * b + j, pb, ntok)

        def sort_copy(dst, src, ntok, d, res_dst, res_src):
            if d == 1:
                P.op("pool", lambda e: e.tensor_copy(out=dst[:, :, 0:ntok], in_=src[:, :, 0:ntok]), reads=[res_src], writes=[res_dst])
            else:
                for half in range(2):
                    cs = slice(8 * half, 8 * half + 8)
                    P.op("pool", lambda e, cs=cs: e.tensor_copy(out=dst[:, cs, 0:ntok].rearrange("p c (r i) -> p c r i", r=d),
                                                            in_=src[:, cs, 0:ntok].rearrange("p c (i r) -> p c r i", r=d)), reads=[res_src], writes=[res_dst])

        def shared_kv(ntok, sample, ti):
            rmsnorm(4, ntok, xn, "xn")
            kst = av(0, [128, DH, TT], BF16)
            vst = av(16384, [128, 4, D], BF16)
            ost = [av(32768, [128, 512], F32), av(34816, [128, 512], F32)]
            vsn_st = av(36864, [NST, D], BF16)
            nsub = (ntok + 127) // 128
            for g in range(3):
                d = DILS[g]
                for kv in range(2):
                    for hb in range(4):
                        c0 = g * 4096 + kv * 2048 + hb * 512
                        bw, rw = wload([(w_kv[:, c0:c0 + 512], 16, 512, 512, 0)], "w_kv")
                        vw = wview(bw, 16, 512)
                        dst_o = (nk if kv == 0 else nv)[g]
                        if kv == 0:
                            def evac(j, pb, hb=hb, g=g, d=d):
                                if sample:
                                    P.op("act", lambda e: e.activation(out=kss[:, g, 4 * hb + j, :], in_=ps[pb][:, 0:ntok], func=AF.Copy), reads=[f"ps{pb}"], writes=["kss"])
                                else:
                                    P.op("act", lambda e: e.activation(out=kst[:, 4 * hb + j, :].rearrange("p (r i) -> p r i", r=d),
                                                                       in_=ps[pb][:, 0:ntok].rearrange("p (i r) -> p r i", r=d), func=AF.Copy), reads=[f"ps{pb}"], writes=["kst"])
                            proj_fm(vw, rw, 0, 4, lambda kc: xn[:, kc, 0:ntok], ntok, evac, [0, 1], ["xn"])
                        if sample:
                            for kc in range(16):
                                P.op("pe", lambda e, kc=kc, vw=vw: e.matmul(ps[4][0:NST, :], lhsT=xn[:, kc, 0:NST], rhs=vw[:, kc, :], start=(kc == 0), stop=(kc == 15)),
                                     reads=[rw, "xn"], writes=["ps4"], sig=(kc == 15))
                            if kv == 1:
                                P.op("act", lambda e, g=g, hb=hb: e.activation(out=vsn_st[:, 512 * hb:512 * hb + 512], in_=ps[4][0:NST, :], func=AF.Copy), reads=["ps4"], writes=["vsn_st"])
                                if hb == 3:
                                    P.dma("sp", Vsd[g], vsn_st[:, :], "vsn_st", reads=["vsn_st"], writes=["Vsd"])
                            P.op("act", lambda e: e.activation(out=ost[0][0:NST, :], in_=ps[4][0:NST, :], func=AF.Copy), reads=["ps4"], writes=["ost0"])
                            dso = (nks if kv == 0 else nvs)[g]
                            for s_ in range(NS):
                                P.dma("sp", dso[s_, WINS[g] - 4:WINS[g], 512 * hb:512 * hb + 512], ost[0][4 * s_:4 * s_ + 4, :], "ost0", reads=["ost0"])
                        else:
                            for s_ in range(nsub):
                                t0 = TT * ti + 128 * s_
                                need_out = t0 >= T - WO[g]
                                if kv == 0 and not need_out:
                                    continue
                                pb = 4 + s_ % 2
                                for kc in range(16):
                                    P.op("pe", lambda e, kc=kc, s_=s_, pb=pb, vw=vw: e.matmul(ps[pb][:, :], lhsT=xn[:, kc, 128 * s_:128 * s_ + 128], rhs=vw[:, kc, :], start=(kc == 0), stop=(kc == 15)),
                                         reads=[rw, "xn"], writes=[f"ps{pb}"], sig=(kc == 15))
                                if kv == 1:
                                    P.op("act", lambda e, s_=s_, hb=hb, pb=pb: e.activation(out=vst[:, s_, 512 * hb:512 * hb + 512], in_=ps[pb][:, :], func=AF.Copy), reads=[f"ps{pb}"], writes=["vst"])
                                if need_out:
                                    on = f"ost{s_ % 2}"
                                    P.op("act", lambda e, s_=s_, pb=pb: e.activation(out=ost[s_ % 2][:, :], in_=ps[pb][:, :], func=AF.Copy), reads=[f"ps{pb}"], writes=[on])
                                    r0 = t0 - (T - WO[g])
                                    P.dma("sp", dst_o[r0:r0 + 128, 512 * hb:512 * hb + 512], ost[s_ % 2][:, :], on, reads=[on])
                    if sample:
                        continue
                    if ti == 0 and g < 2:
                        if kv == 0:
                            dbg(f"kst{g}", kst[:, :, :], ["kst"], BF16)
                        else:
                            dbg(f"vst{g}", vst[:, :, :], ["vst"], BF16)
                    n_r = T // d
                    per = TT // d
                    if kv == 0:
                        for h in range(DH):
                            dstv = KTd[g][:, h, :].rearrange("p (r m) -> p r m", r=d)[:, :, per * ti:per * (ti + 1)]
                            P.dma("sp", dstv, kst[:, h, :].rearrange("p (r i) -> p r i", r=d), "kst", reads=["kst"], writes=[f"KTd{g}"])
                    else:
                        pp = 128 // d
                        for s_ in range(nsub):
                            if os.environ.get("KFIXB", ""):
                                t0 = TT * ti + 128 * s_
                                P.dma("sp", Vd[g][t0:t0 + 128, :], vst[:, s_, :], "vst", reads=["vst"], writes=[f"Vd{g}"])
                                continue
                            for r in range(d):
                                sl0 = r * n_r + per * ti + pp * s_
                                P.dma("sp", Vd[g][sl0:sl0 + pp, :], vst[r:128:d, s_, :], "vst", reads=["vst"], writes=[f"Vd{g}"])

        def attention(ntok, sample, ti):
            rmsnorm(1, ntok, xn, "xn")
            qg = av(0, [128, DH, TT], BF16)
            num = av(16384, [128, DH, TT], F32)
            o1 = 49152
            ktl = [av(o1, [128, DH, 128], BF16), av(o1 + 4096, [128, DH, 128], BF16)]
            vtl = [av(o1 + 8192, [128, D], BF16), av(o1 + 12288, [128, D], BF16)]
            pt = [av(o1 + 16384, [128, 512], BF16), av(o1 + 17408, [128, 512], BF16)]
            ktm = vtl[1]
            vsa = av(o1 + 18432, [NST, D], BF16)
            first_acc = [True]
            FIXA = bool(os.environ.get("KFIXA", ""))
            FIXB = bool(os.environ.get("KFIXB", ""))
            if not FIXA:
                P.op("pool", lambda e: e.memset(den_acc[:], 0.0), writes=["den_acc"])
                P.op("pool", lambda e: e.memset(num[:], 0.0), writes=["num"])

            def unit(g, nq, qsl, keytiles, acc_fn, dacc):
                first = (g == 0) and FIXA
                hbn = min(DH, 512 // nq)
                nb = DH // hbn
                nkt = len(keytiles)
                for b in range(nb):
                    ob_ = 2 + b % 2
                    for ki, (kfn, vt, nk_, mk, rl) in enumerate(keytiles):
                        sb_ = ki % 2
                        for hh in range(hbn):
                            h = b * hbn + hh
                            P.op("pe", lambda e, h=h, hh=hh, kfn=kfn, sb_=sb_, nk_=nk_: e.matmul(ps[sb_][0:nk_, hh * nq:(hh + 1) * nq], lhsT=kfn(h), rhs=qg[:, h, qsl], start=True, stop=True),
                                 reads=rl + ["qg"], writes=[f"ps{sb_}"], sig=(hh == hbn - 1))
                        pn = f"pt{sb_}"
                        P.op("act", lambda e, sb_=sb_, nk_=nk_: e.activation(out=pt[sb_][0:nk_, 0:hbn * nq], in_=ps[sb_][0:nk_, 0:hbn * nq], func=AF.Exp), reads=[f"ps{sb_}"], writes=[pn])
                        if mk is not None:
                            P.op("dve", lambda e, sb_=sb_, nk_=nk_, mk=mk: e.tensor_tensor(out=pt[sb_][0:nk_, 0:hbn * nq].rearrange("p (h q) -> p h q", h=hbn),
                                                                                           in0=pt[sb_][0:nk_, 0:hbn * nq].rearrange("p (h q) -> p h q", h=hbn),
                                                                                           in1=mk.unsqueeze(1).broadcast_to([nk_, hbn, nq]), op=ALU.mult), reads=[pn], writes=[pn])
                    for hh in range(hbn):
                        h = b * hbn + hh
                        for ki, (kfn, vt, nk_, mk, rl) in enumerate(keytiles):
                            sb_ = ki % 2
                            P.op("pe", lambda e, h=h, hh=hh, vt=vt, sb_=sb_, nk_=nk_, ki=ki, ob_=ob_: e.matmul(ps[ob_][:, hh * nq:(hh + 1) * nq], lhsT=vt[0:nk_, 128 * h:128 * h + 128], rhs=pt[sb_][0:nk_, hh * nq:(hh + 1) * nq],
                                                                                                         start=(ki == 0), stop=(ki == nkt - 1)),
                                 reads=rl + [f"pt{sb_}"], writes=[f"ps{ob_}"], sig=False)
                        for ki, (kfn, vt, nk_, mk, rl) in enumerate(keytiles):
                            sb_ = ki % 2
                            P.op("pe", lambda e, h=h, hh=hh, sb_=sb_, nk_=nk_, ki=ki, b=b: e.matmul(ps[6][0:DH, 0:nq], lhsT=onesel[0:nk_, h, :], rhs=pt[sb_][0:nk_, hh * nq:(hh + 1) * nq],
                                                                                                 start=(ki == 0 and b == 0 and hh == 0), stop=(ki == nkt - 1 and b == nb - 1 and hh == hbn - 1)),
                                 reads=[f"pt{sb_}"], writes=["ps6"], sig=True)
                    acc = acc_fn(b * hbn, hbn)
                    if first:
                        P.op("dve", lambda e, acc=acc, ob_=ob_: e.tensor_copy(out=acc, in_=ps[ob_][:, 0:hbn * nq].rearrange("p (h q) -> p h q", h=hbn)),
                             reads=[f"ps{ob_}"], writes=["num"])
                    else:
                        P.op("dve", lambda e, acc=acc, ob_=ob_: e.tensor_tensor(out=acc, in0=ps[ob_][:, 0:hbn * nq].rearrange("p (h q) -> p h q", h=hbn), in1=acc, op=ALU.add),
                             reads=[f"ps{ob_}", "num"], writes=["num"])
                if first:
                    P.op("dve", lambda e: e.tensor_copy(out=dacc, in_=ps[6][0:DH, 0:nq]), reads=["ps6"], writes=["den_acc"])
                else:
                    P.op("dve", lambda e: e.tensor_tensor(out=dacc, in0=ps[6][0:DH, 0:nq], in1=dacc, op=ALU.add), reads=["ps6", "den_acc"], writes=["den_acc"])

            for g in range(3):
                d = DILS[g]
                for hb in range(4):
                    c0 = g * 2048 + hb * 512
                    bw, rw = wload([(w_q[:, c0:c0 + 512], 16, 512, 512, 0)], "w_q")
                    vw = wview(bw, 16, 512)

                    def evac(j, pb, hb=hb, d=d):
                        dd = 1 if sample else d
                        P.op("act", lambda e: e.activation(out=qg[:, 4 * hb + j, 0:ntok].rearrange("p (r i) -> p r i", r=dd),
                                                           in_=ps[pb][:, 0:ntok].rearrange("p (i r) -> p r i", r=dd), func=AF.Copy, scale=float(HD) ** -0.5), reads=[f"ps{pb}"], writes=["qg"])
                    proj_fm(vw, rw, 0, 4, lambda kc: xn[:, kc, 0:ntok], ntok, evac, [4, 5], ["xn"])
                if not sample:
                    n_r = T // d
                    per = TT // d
                    nq = min(128, per)
                    ucount = TT // nq
                    lrr = [0]

                    def ldtile(r_, ms, nk_):
                        i = lrr[0] % 2
                        lrr[0] += 1
                        slot0 = r_ * n_r + ms
                        if FIXB:
                            vrows = Vd[g][ms:ms + nk_, :] if d == 1 else Vd[g][ms * d + r_:(ms + nk_ - 1) * d + r_ + 1:d, :]
                        else:
                            vrows = Vd[g][slot0:slot0 + nk_, :]
                        for h4 in range(4):
                            P.dma("sp", ktl[i][:, 4 * h4:4 * h4 + 4, 0:nk_], KTd[g][:, 4 * h4:4 * h4 + 4, slot0:slot0 + nk_], f"ktl{i}", reads=[f"KTd{g}"], writes=[f"ktl{i}"])
                        P.dma("sp", vtl[i][0:nk_, :], vrows, f"vtl{i}", reads=[f"Vd{g}"], writes=[f"vtl{i}"])
                        return (lambda h, i=i, nk_=nk_: ktl[i][:, h, 0:nk_]), vtl[i], [f"ktl{i}", f"vtl{i}"]

                    for u in range(ucount):
                        r = (u * nq) // per
                        i0 = (u * nq) % per
                        m0 = per * ti + i0
                        kts = []
                        pm0 = max(0, m0 - 128)
                        if m0 - pm0 > 0:
                            kf, vt, rl = ldtile(r, pm0, m0 - pm0)
                            mk = masks[0:m0 - pm0, 1, 0:nq] if (m0 - pm0) == 128 else None
                            kts.append((kf, vt, m0 - pm0, mk, rl))
                        kf, vt, rl = ldtile(r, m0, nq)
                        kts.append((kf, vt, nq, masks[0:nq, 0, 0:nq], rl))
                        if d == 1:
                            accf = lambda h0, hbn, u=u: num[:, h0:h0 + hbn, u * nq:(u + 1) * nq]
                            dacc = den_acc[:, u * nq:(u + 1) * nq]
                        else:
                            accf = lambda h0, hbn, r=r, i0=i0: num[:, h0:h0 + hbn, :].rearrange("p h (i r) -> p h r i", r=d)[:, :, r, i0:i0 + nq]
                            dacc = den_acc[:, :].rearrange("p (i r) -> p r i", r=d)[:, r, i0:i0 + nq]
                        unit(g, nq, slice(u * nq, (u + 1) * nq), kts, accf, dacc)
                    if ti == 0 and g < 2:
                        dbg(f"num_g{g}", num[:, :, :], ["num"])
                        dbg(f"den_g{g}", den_acc[:, :], ["den_acc"])
                else:
                    W = WINS[g]
                    P.dma("sp", vsa[:, :], Vsd[g], "vsa", reads=["Vsd"], writes=["vsa"])
                    for s_ in range(NS):
                        tl = [None] if g == 0 else list(range(4))
                        for t_ in tl:
                            rows = ck[g][s_, 0:128, :] if g == 0 else ck[g][s_, t_:W:d, :]
                            rowsv = cv[g][s_, 0:128, :] if g == 0 else cv[g][s_, t_:W:d, :]
                            P.dma("pool", ktm[:, :], rows, "vtl1", writes=["vtl1"])
                            P.dma("pool", vtl[0][:, :], rowsv, "vtl0", writes=["vtl0"])
                            for h4 in range(4):
                                for j in range(4):
                                    h = 4 * h4 + j
                                    P.op("pe", lambda e, h=h, j=j: e.transpose(psb(7)[:, 128 * j:128 * j + 128], ktm[:, 128 * h:128 * h + 128], masks[:, 2, :]), reads=["vtl1"], writes=["ps7"], sig=(j == 3))
                                P.op("act", lambda e, h4=h4: e.activation(out=ktl[0][:, 4 * h4:4 * h4 + 4, :], in_=psb(7)[:, 0:512].rearrange("p (j k) -> p j k", j=4), func=AF.Copy), reads=["ps7"], writes=["ktl0"])
                            if g == 0:
                                nq, qsl = 4, slice(4 * s_, 4 * s_ + 4)
                                mk1 = masks[:, 1, 0:4]
                                mk2 = masks[0:NST, 3, 4 * s_:4 * s_ + 4]
                            else:
                                nq, qsl = 1, slice(4 * s_ + t_, 4 * s_ + t_ + 1)
                                mk1 = None
                                mk2 = masks[0:NST, 2, 4 * s_ + t_:4 * s_ + t_ + 1]
                            kts = [((lambda h: ktl[0][:, h, :]), vtl[0], 128, mk1, ["ktl0", "vtl0"]),
                                   ((lambda h, g=g: kss[:, g, h, :]), vsa, NST, mk2, ["kss", "vsa"])]
                            accf = lambda h0, hbn, qsl=qsl: num[:, h0:h0 + hbn, qsl]
                            unit(g, nq, qsl, kts, accf, den_acc[:, qsl])
            if not sample and ti == 0:
                dbg("num", num[:, :, :], ["num"])
                dbg("den", den_acc[:, :], ["den_acc"])
            P.op("dve", lambda e: e.reciprocal(out=rden[:, 0:ntok], in_=den_acc[:, 0:ntok]), reads=["den_acc"], writes=["rden"])
            oT = qg
            for h in range(DH):
                pb = h % 2
                P.op("pe", lambda e, h=h, pb=pb: e.matmul(ps[pb][:, 0:ntok], lhsT=sel[0:DH, h, :], rhs=rden[:, 0:ntok], start=True, stop=True), reads=["rden"], writes=[f"ps{pb}"])
                P.op("dve", lambda e, h=h, pb=pb: e.tensor_tensor(out=oT[:, h, 0:ntok], in0=num[:, h, 0:ntok], in1=ps[pb][:, 0:ntok], op=ALU.mult), reads=["num", f"ps{pb}", "qg"], writes=["oT", "qg"])
            for b in range(4):
                bw, rw = wload([(w_o[:, 512 * b:512 * b + 512], 16, 512, 512, 0)], "w_o")
                vw = wview(bw, 16, 512)
                proj_fm(vw, rw, 0, 4, lambda kc: oT[:, kc, 0:ntok], ntok, lambda j, pb, b=b: add_resid(4 * b + j, pb, ntok), [4, 5], ["oT"])

        def final_out(dst_rows, ntok):
            yf = av(0, [128, NCH, TT], F32)
            rmsnorm(5, ntok, None, "yf", out_f32=yf)
            nsub = (ntok + 127) // 128
            for s_ in range(nsub):
                m = min(128, ntok - 128 * s_)
                stg = av(32768 + 8192 * (s_ % 2), [128, D], F32)
                rn = f"yst{s_ % 2}"
                for b in range(4):
                    pb = 4 + b % 2
                    for j in range(4):
                        c_ = 4 * b + j
                        P.op("pe", lambda e, j=j, c_=c_, pb=pb, m=m, s_=s_: e.transpose(ps[pb][0:m, 128 * j:128 * j + 128], yf[:, c_, 128 * s_:128 * s_ + m], identf[:, :]),
                             reads=["yf"], writes=[f"ps{pb}"], sig=(j == 3))
                    P.op("act", lambda e, b=b, pb=pb, m=m, stg=stg: e.activation(out=stg[0:m, 512 * b:512 * b + 512], in_=ps[pb][0:m, :], func=AF.Copy), reads=[f"ps{pb}"], writes=[rn])
                P.dma("sp", dst_rows[128 * s_:128 * s_ + m, :], stg[0:m, :], rn, reads=[rn])

        import os
        KSTOP = int(os.environ.get("KSTOP", "99"))
        tiles = [("p", ti) for ti in range(NT)] + [("s", 0)]
        if os.environ.get("KTILES"):
            tiles = [t for t in tiles if t[0] in os.environ["KTILES"]]
        for kind, ti in tiles:
            sample = kind == "s"
            ntok = NST if sample else TT
            if sample:
                P.dma("sp", cosT[:, 0:NST], cd["coss"], "rope", writes=["rope"])
                P.dma("sp", sinT[:, 0:NST], cd["sins"], "rope", writes=["rope"])
                load_tokens(xs, NST)
            else:
                P.dma("sp", cosT[:, :], cd["cosp"][:, TT * ti:TT * ti + TT], "rope", writes=["rope"])
                P.dma("sp", sinT[:, :], cd["sinp"][:, TT * ti:TT * ti + TT], "rope", writes=["rope"])
                load_tokens(xp[TT * ti:TT * ti + TT, :], TT)
            if not sample and ti == 0:
                dbg("hT0", hT[:, :, :], ["hT"])
            P.barrier()
            if KSTOP >= 1:
                retention(ntok, 4 if sample else 128, 4, sample, ti)
            if not sample and ti == 0:
                dbg("hT1", hT[:, :, :], ["hT"])
            P.barrier()
            if KSTOP >= 2:
                ffn(0, ntok)
            if not sample and ti == 0:
                dbg("hT2", hT[:, :, :], ["hT"])
            P.barrier()
            if KSTOP >= 3:
                shared_kv(ntok, sample, ti)
            P.barrier()
            if KSTOP >= 4:
                attention(ntok, sample, ti)
            if not sample and ti == 0:
                dbg("hT3", hT[:, :, :], ["hT"])
            if sample:
                dbg("hT3s", hT[:, :, 0:NST], ["hT"])
            P.barrier()
            if KSTOP >= 5:
                ffn(1, ntok)
            P.barrier()
            final_out(y_s if sample else y_p[TT * ti:TT * ti + TT, :], ntok)
            P.barrier()
        P.final_wait("sp")

        with nc.Block() as block:
            @block.tensor
            def _(e):
                for f in P.q["pe"]:
                    f(e)

            @block.vector
            def _(e):
                for f in P.q["dve"]:
                    f(e)

            @block.scalar
            def _(e):
                for f in P.q["act"]:
                    f(e)

            @block.gpsimd
            def _(e):
                for f in P.q["pool"]:
                    f(e)

            @block.sync
            def _(e):
                for f in P.q["sp"]:
                    f(e)
    return nc, cst


_CACHE = {}


def make_inputs(NT, c, inputs, cst):
    T = TT * NT
    f = lambda a: np.ascontiguousarray(np.asarray(a, dtype=np.float32))
    b = c % 2
    m = {}
    m["xp"] = f(inputs["x_prompt"][b][:T])
    m["xs"] = f(inputs["x_sample"][NS * c:NS * c + NS]).reshape(NST, D)
    m["state"] = f(inputs["state_ret"][0][NS * c:NS * c + NS])
    for g, w in enumerate(WINS):
        m[f"ck{g}"] = f(inputs[f"cache_k_w{w}"][NS * c:NS * c + NS]).reshape(NS, w, D)
        m[f"cv{g}"] = f(inputs[f"cache_v_w{w}"][NS * c:NS * c + NS]).reshape(NS, w, D)
    vecs = [inputs["norm_mix"][0], inputs["norm_mix"][1], inputs["norm_ffn"][0], inputs["norm_ffn"][1], inputs["kv_norm"], inputs["norm_final"]]
    m["nvec"] = f(np.stack([np.asarray(v).reshape(NCH, 128).T for v in vecs], axis=1))
    m["gnv"] = f(np.asarray(inputs["ret_gn"][0]).reshape(32, 128).T)
    m["w_in"] = f(inputs["ret_w_in"][0]); m["w_out"] = f(inputs["ret_w_out"][0]); m["w_kv"] = f(inputs["w_kv"])
    m["w_q"] = f(inputs["dil_w_q"][0]); m["w_o"] = f(inputs["dil_w_o"][0])
    m["w1"] = f(inputs["ffn_w1"]); m["w3"] = f(inputs["ffn_w3"]); m["w2"] = f(inputs["ffn_w2"])
    for k, v in cst.items():
        m["c_" + k] = np.ascontiguousarray(v)
    return m


def run(inputs, NT):
    if NT not in _CACHE:
        _CACHE[NT] = build(NT)
    nc, cst = _CACHE[NT]
    T = TT * NT
    import os
    ncores = int(os.environ.get("KCORES", "8"))
    in_maps = [make_inputs(NT, c, inputs, cst) for c in range(ncores)]
    res = run_bass_kernel_spmd(nc, in_maps, core_ids=list(range(ncores)))
    R = list(res.results)
    global LAST_R
    LAST_R = R
    while len(R) < 8:
        R.append(R[len(R) % ncores])
    B = 2
    WO = [min(w, T) for w in WINS]
    y_prompt = np.stack([R[b]["y_p"] for b in range(B)]).astype(np.float32)
    y_sample = np.concatenate([R[c]["y_s"].reshape(NS, 4, D) for c in range(8)]).astype(np.float32)
    st_p = np.stack([R[b]["st_p"] for b in range(B)])[None].astype(np.float32)
    st_s = np.concatenate([R[c]["st_s"] for c in range(8)])[None].astype(np.float32)
    outs = [y_prompt, y_sample, st_p, st_s]
    for g in range(3):
        outs.append(np.stack([R[b][f"nk{g}"].reshape(WO[g], DH, HD) for b in range(B)]).astype(np.float32))
        outs.append(np.stack([R[b][f"nv{g}"].reshape(WO[g], DH, HD) for b in range(B)]).astype(np.float32))
    for g in range(3):
        outs.append(np.concatenate([R[c][f"nks{g}"].reshape(NS, WINS[g], DH, HD) for c in range(8)]).astype(np.float32))
        outs.append(np.concatenate([R[c][f"nvs{g}"].reshape(NS, WINS[g], DH, HD) for c in range(8)]).astype(np.float32))
    return tuple(outs)


def kernel(**inputs):
    NT = np.asarray(inputs["x_prompt"]).shape[1] // TT
    return run(inputs, NT)
```
